# Optimizing a Trainium2 kernel written in Bass

```python
import math
import jax, jax.numpy as jnp
from jax import lax
import numpy as np

D_MODEL = 4096
BATCH = 8
SEQ = 2048
DEPTH = 1
DEC_BATCH = 32
DEC_SEQ = 64
PAST_LEN = 2048

CHUNK = 64
N_HEADS = 8
HEAD_DIM = 128
V_DIM = 2 * HEAD_DIM
QK_W = N_HEADS * 2 * HEAD_DIM
ATTN_W = N_HEADS * V_DIM
POOL_WINDOWS = (2, 4, 8, 16)
POOL_GROUPS = len(POOL_WINDOWS)
POOL_W = D_MODEL // 2
POOL_GROUP_W = POOL_W // POOL_GROUPS
POOL_HIST = max(POOL_WINDOWS) - 1
D_FF = 4 * D_MODEL
IN_COLS = 2 * QK_W + ATTN_W + POOL_W + 2 * D_MODEL
Q_BLOCK = 128
NORM_EPS = 1e-6
SUBLN_EPS = 1e-5
NEG_INF = -1e30

kernel_name = 'streaming_diffattn_pool_hybrid_step'


def rmsnorm(x, g, eps=NORM_EPS):
    xf = x.astype(jnp.float32)
    y = xf * lax.rsqrt(jnp.mean(xf * xf, axis=-1, keepdims=True) + eps)
    return (y * g.astype(jnp.float32)).astype(x.dtype)


def diff_attend(q, pos_q, parts, lam):
    logits = jnp.concatenate(
        [jnp.einsum('bqhcd,bkhcd->cbhqk', q, k, preferred_element_type=jnp.float32) for k, _, _ in parts],
        axis=-1) * (HEAD_DIM ** -0.5)
    pos_k = jnp.concatenate([pk for _, _, pk in parts])
    mask = (pos_k[None, :] // CHUNK) <= (pos_q[:, None] // CHUNK)
    p = jax.nn.softmax(jnp.where(mask, logits, NEG_INF), axis=-1)
    w = p[0] - lam * p[1]
    terms = []
    start = 0
    for k, v, _ in parts:
        n = k.shape[1]
        terms.append(jnp.einsum('bhqk,bkhe->bqhe', w[..., start:start + n], v.astype(jnp.float32)))
        start += n
    return sum(terms[1:], terms[0])


def prompt_attend(q, k, v, pos, lam):
    B, T = q.shape[:2]
    nb = T // Q_BLOCK
    qb = q.reshape(B, nb, Q_BLOCK, N_HEADS, 2, HEAD_DIM).swapaxes(0, 1)
    pb = pos.reshape(nb, Q_BLOCK)
    ob = lax.map(lambda a: diff_attend(a[0], a[1], [(k, v, pos)], lam), (qb, pb))
    return ob.swapaxes(0, 1).reshape(B, T, N_HEADS, V_DIM)


def multiscale_pool(u_pad, pos):
    B, L, _ = u_pad.shape
    T = L - POOL_HIST
    uf = u_pad.astype(jnp.float32)
    csum = jnp.concatenate([jnp.zeros((B, 1, POOL_W), jnp.float32), jnp.cumsum(uf, axis=1)], axis=1)
    groups = []
    for g, w in enumerate(POOL_WINDOWS):
        c0 = g * POOL_GROUP_W
        c1 = c0 + POOL_GROUP_W
        win_sum = csum[:, POOL_HIST + 1:, c0:c1] - csum[:, POOL_HIST + 1 - w:POOL_HIST + 1 - w + T, c0:c1]
        count = jnp.minimum(pos + 1, w).astype(jnp.float32)[None, :, None]
        groups.append(win_sum / count - uf[:, POOL_HIST:, c0:c1])
    return jnp.stack(groups, axis=2)


def trunk_layer(x, pos, pool_hist, kv_cache, lam_init, g_norm1, w_in, g_q, g_k, lambda_q1, lambda_k1,
                lambda_q2, lambda_k2, g_subln, w_attn_out, w_pool, pool_scale, w_pool_out, w_o,
                g_norm2, w_up, w_down):
    B, T, _ = x.shape
    xn = rmsnorm(x, g_norm1)
    h = xn @ w_in
    o1 = QK_W
    o2 = o1 + QK_W
    o3 = o2 + ATTN_W
    o4 = o3 + POOL_W
    o5 = o4 + D_MODEL
    q, k, v, u, ga, gb = jnp.split(h, [o1, o2, o3, o4, o5], axis=-1)
    q = rmsnorm(q.reshape(B, T, N_HEADS, 2, HEAD_DIM), g_q)
    k = rmsnorm(k.reshape(B, T, N_HEADS, 2, HEAD_DIM), g_k)
    v = v.reshape(B, T, N_HEADS, V_DIM)
    lam = (jnp.exp(jnp.sum(lambda_q1.astype(jnp.float32) * lambda_k1.astype(jnp.float32)))
           - jnp.exp(jnp.sum(lambda_q2.astype(jnp.float32) * lambda_k2.astype(jnp.float32))) + lam_init)
    if kv_cache is None:
        o = prompt_attend(q, k, v, pos, lam)
    else:
        ck, cv = kv_cache
        pos_c = jnp.arange(ck.shape[1])
        o = diff_attend(q, pos, [(ck, cv, pos_c), (k, v, pos)], lam)
    o = rmsnorm(o, g_subln, SUBLN_EPS) * (1.0 - lam_init)
    y_a = o.reshape(B, T, ATTN_W).astype(x.dtype) @ w_attn_out
    u_pad = jnp.concatenate([pool_hist.astype(u.dtype), u], axis=1)
    pooled = multiscale_pool(u_pad, pos)
    mixed = jnp.einsum('btgc,gcd->btgd', pooled, w_pool.astype(jnp.float32)).reshape(B, T, POOL_W)
    y_b = (mixed * pool_scale.astype(jnp.float32)).astype(x.dtype) @ w_pool_out
    merged = jax.nn.sigmoid(ga) * y_a + jax.nn.sigmoid(gb) * y_b
    x = x + merged @ w_o
    hn = rmsnorm(x, g_norm2)
    x = x + jnp.square(jax.nn.relu(hn @ w_up)) @ w_down
    return x, k, v, u_pad[:, -POOL_HIST:]


def setup_inputs(seed: int = 0) -> dict:
    key = jax.random.key(seed)
    ks = jax.random.split(key, 24)

    def nrm(k, shape, scale):
        return jax.random.normal(k, shape, jnp.float32) * scale

    def gain(k, shape, s=0.05):
        return 1.0 + nrm(k, shape, s)

    return {
        'x_prompt': nrm(ks[0], (BATCH, SEQ, D_MODEL), 1.0),
        'x_sample': nrm(ks[1], (DEC_BATCH, DEC_SEQ, D_MODEL), 1.0),
        'cache_k': nrm(ks[2], (DEPTH, DEC_BATCH, PAST_LEN, N_HEADS, 2, HEAD_DIM), 1.0),
        'cache_v': nrm(ks[3], (DEPTH, DEC_BATCH, PAST_LEN, N_HEADS, V_DIM), 1.0),
        'state_pool': nrm(ks[4], (DEPTH, DEC_BATCH, POOL_HIST, POOL_W), 1.0),
        'g_norm1': gain(ks[5], (DEPTH, D_MODEL)),
        'w_in': nrm(ks[6], (DEPTH, D_MODEL, IN_COLS), D_MODEL ** -0.5),
        'g_q': gain(ks[7], (DEPTH, HEAD_DIM)),
        'g_k': gain(ks[8], (DEPTH, HEAD_DIM)),
        'lambda_q1': nrm(ks[9], (DEPTH, HEAD_DIM), 0.1),
        'lambda_k1': nrm(ks[10], (DEPTH, HEAD_DIM), 0.1),
        'lambda_q2': nrm(ks[11], (DEPTH, HEAD_DIM), 0.1),
        'lambda_k2': nrm(ks[12], (DEPTH, HEAD_DIM), 0.1),
        'g_subln': gain(ks[13], (DEPTH, V_DIM)),
        'w_attn_out': nrm(ks[14], (DEPTH, ATTN_W, D_MODEL), ATTN_W ** -0.5),
        'w_pool': nrm(ks[15], (DEPTH, POOL_GROUPS, POOL_GROUP_W, POOL_GROUP_W), POOL_GROUP_W ** -0.5),
        'pool_scale': gain(ks[16], (DEPTH, POOL_W), 0.1),
        'w_pool_out': nrm(ks[17], (DEPTH, POOL_W, D_MODEL), POOL_W ** -0.5),
        'w_o': nrm(ks[18], (DEPTH, D_MODEL, D_MODEL), D_MODEL ** -0.5),
        'g_norm2': gain(ks[19], (DEPTH, D_MODEL)),
        'w_up': nrm(ks[20], (DEPTH, D_MODEL, D_FF), D_MODEL ** -0.5),
        'w_down': nrm(ks[21], (DEPTH, D_FF, D_MODEL), D_FF ** -0.5),
    }


def reference(x_prompt, x_sample, cache_k, cache_v, state_pool, g_norm1, w_in, g_q, g_k, lambda_q1,
              lambda_k1, lambda_q2, lambda_k2, g_subln, w_attn_out, w_pool, pool_scale, w_pool_out, w_o,
              g_norm2, w_up, w_down):
    past = cache_k.shape[2]
    pos_p = jnp.arange(x_prompt.shape[1])
    pos_s = past + jnp.arange(x_sample.shape[1])
    hist_p = jnp.zeros((x_prompt.shape[0], POOL_HIST, POOL_W), x_prompt.dtype)
    y_p, y_s = x_prompt, x_sample
    kp_l, vp_l, sp_l, ks_l, vs_l, ss_l = [], [], [], [], [], []
    for l in range(DEPTH):
        lam_init = 0.8 - 0.6 * math.exp(-0.3 * l)
        w = (g_norm1[l], w_in[l], g_q[l], g_k[l], lambda_q1[l], lambda_k1[l], lambda_q2[l], lambda_k2[l],
             g_subln[l], w_attn_out[l], w_pool[l], pool_scale[l], w_pool_out[l], w_o[l], g_norm2[l],
             w_up[l], w_down[l])
        y_p, kp, vp, sp = trunk_layer(y_p, pos_p, hist_p, None, lam_init, *w)
        y_s, ksm, vsm, ssm = trunk_layer(y_s, pos_s, state_pool[l], (cache_k[l], cache_v[l]), lam_init, *w)
        kp_l.append(kp)
        vp_l.append(vp)
        sp_l.append(sp)
        ks_l.append(ksm)
        vs_l.append(vsm)
        ss_l.append(ssm)
    return (y_p, y_s, jnp.stack(kp_l), jnp.stack(vp_l), jnp.stack(sp_l),
            jnp.stack(ks_l), jnp.stack(vs_l), jnp.stack(ss_l))
```

```python
import numpy as np
import concourse.bass as bass
import concourse.mybir as mybir
from concourse.bass_utils import run_bass_kernel_spmd
from contextlib import ExitStack

F32 = mybir.dt.float32
BF16 = mybir.dt.bfloat16
AF = mybir.ActivationFunctionType
ALU = mybir.AluOpType
AX = mybir.AxisListType

D = 4096
NST = 18
NSUB = 2
G = NSUB * 128
TOK = NST * 128
NSLOT = 6
NKV = 8
SLOT_E = 4096
LAM_INIT = 0.8 - 0.6 * 1.0
EPS = 1e-6
SUB_EPS = 1e-5
WINS = (2, 4, 8, 16)


class Res:
    __slots__ = ("name", "lastw", "readers", "const")

    def __init__(self, name, const=False):
        self.name = name
        self.lastw = None
        self.readers = []
        self.const = const


class Prog:
    ENG = ("pe", "act", "dve", "pool", "sp")

    def __init__(self, ndma=8):
        self.ops = {e: [] for e in self.ENG}
        self.seen = {e: {} for e in self.ENG}
        self.dma_n = {e: 0 for e in self.ENG}
        self.dma_last = {}
        self.ndma = ndma
        self.dry = False

    def op(self, eng, fn, reads=(), writes=(), dma=False):
        if self.dry:
            return None
        deps = []
        for r in reads:
            if r.lastw is not None:
                deps.append(r.lastw)
        for w in writes:
            if w.lastw is not None:
                deps.append(w.lastw)
            deps.extend(w.readers)
        idx = len(self.ops[eng])
        if dma:
            slot = self.dma_n[eng] % self.ndma
            use = self.dma_n[eng] // self.ndma + 1
            self.dma_n[eng] += 1
            prev = self.dma_last.get((eng, slot))
            if prev is not None:
                deps.append(prev)
            ev = ("dma", (eng, slot), 16 * use)
            self.dma_last[(eng, slot)] = ev
        else:
            ev = ("eng", eng, idx)
        best = {}
        for d in deps:
            kind, key, val = d
            if kind == "eng" and key == eng and eng == "pe":
                continue
            k = (kind, key)
            if best.get(k, -1) < val:
                best[k] = val
        waits = []
        seen = self.seen[eng]
        for k, val in best.items():
            if seen.get(k, -1) >= val:
                continue
            seen[k] = val
            waits.append((k[0], k[1], val))
            if k[0] == "eng":
                self.ops[k[1]][val]["inc"] = True
        self.ops[eng].append({"fn": fn, "waits": waits, "inc": False, "dma": ev if dma else None})
        for r in reads:
            if not r.const:
                r.readers.append(ev)
        for w in writes:
            w.lastw = ev
            w.readers = []
        return ev

    def finish(self):
        deps = list(self.dma_last.values())
        best = {}
        for kind, key, val in deps:
            best[(kind, key)] = max(best.get((kind, key), -1), val)
        waits = [(k[0], k[1], v) for k, v in best.items()]
        for e in self.ENG:
            if e != "sp" and self.ops[e]:
                i = len(self.ops[e]) - 1
                if self.ops[e][i]["dma"] is None:
                    self.ops[e][i]["inc"] = True
                    waits.append(("eng", e, i))
        self.ops["sp"].append({"fn": None, "waits": waits, "inc": False, "dma": None})

    def emit(self, nc, block, sems, dma_sems):
        for e in self.ENG:
            c = 0
            for o in self.ops[e]:
                if o["inc"]:
                    c += 1
                    o["val"] = c
        ops = self.ops

        def run(e, engine):
            for o in ops[e]:
                for kind, key, val in o["waits"]:
                    if kind == "eng":
                        engine.wait_ge(sems[key], ops[key][val]["val"])
                    else:
                        engine.wait_ge(dma_sems[key], val)
                if o["fn"] is None:
                    continue
                ins = o["fn"](engine)
                if o["dma"] is not None:
                    ins.then_inc(dma_sems[o["dma"][1]], 16)
                elif o["inc"]:
                    ins.then_inc(sems[e], 1)

        @block.tensor
        def _(t):
            run("pe", t)

        @block.scalar
        def _(s):
            run("act", s)

        @block.vector
        def _(v):
            run("dve", v)

        @block.gpsimd
        def _(g):
            run("pool", g)

        @block.sync
        def _(sy):
            run("sp", sy)


class WStream:
    def __init__(self, P, slots, slot_res, sched=None, scratch=None, npass=None):
        self.P = P
        self.slots = slots
        self.res = slot_res
        self.sched = sched
        self.rec = []
        self.i = 0
        self.issued = 0
        self.scratch = scratch
        self.npass = npass
        self.scr_res = [Res(f"scr{i}") for i in range(npass)] if npass else None

    def _issue(self, j):
        w, r0, nk, c0, ncol = self.sched[j]
        s = j % NSLOT
        if self.scratch is None or j < self.npass:
            dst = self.slots[s][:, 0:nk * ncol].rearrange("p (k n) -> p k n", k=nk)
            src = w[r0 * 128:(r0 + nk) * 128, c0:c0 + ncol].rearrange("(k p) n -> p k n", p=128)
            self.P.op("pool", lambda e, d=dst, s_=src: e.dma_start(out=d, in_=s_), reads=(), writes=(self.res[s],), dma=True)
        else:
            jj = j % self.npass
            assert self.sched[jj][1:] == self.sched[j][1:]
            sd = self.scratch[jj // 240][(jj % 240) * 128:(jj % 240 + 1) * 128, 0:nk * ncol]
            ds = self.slots[s][:, 0:nk * ncol]
            self.P.op("pool", lambda e, d=ds, s_=sd: e.dma_start(out=d, in_=s_), reads=(self.scr_res[jj],), writes=(self.res[s],), dma=True)

    def get(self, w, r0, nk, c0, ncol):
        desc = (w, r0, nk, c0, ncol)
        if self.sched is None:
            self.rec.append(desc)
            self.i += 1
            return None, None
        j = self.i
        d = self.sched[j]
        assert d[1:] == desc[1:], (d[1:], desc[1:])
        while self.issued < len(self.sched) and self.issued < j + NSLOT:
            self._issue(self.issued)
            self.issued += 1
        self.i += 1
        s = j % NSLOT
        if self.scratch is not None and j < self.npass:
            sd = self.scratch[j // 240][(j % 240) * 128:(j % 240 + 1) * 128, 0:nk * ncol]
            ss = self.slots[s][:, 0:nk * ncol]
            self.P.op("sp", lambda e, d=sd, s_=ss: e.dma_start(out=d, in_=s_), reads=(self.res[s],), writes=(self.scr_res[j],), dma=True)
        view = self.slots[s][:, 0:nk * ncol].rearrange("p (k n) -> p k n", k=nk)
        return view, self.res[s]


def build_core_program(nc, T, P, ws):
    ident = T["ident"]
    CONST = T["const_res"]
    banks = T["banks"]
    bank_res = T["bank_res"]
    B1, b1_res = T["B1"], T["b1_res"]
    BIG, big_res = T["BIG"], T["big_res"]
    SGA0, SGB0, OT0, MIX0, PLD0, QT0 = 0, 32, 64, 80, 96, 112
    x_all, y_all, kv_all, pool_all = T["x_all"], T["y_all"], T["kv_all"], T["pool_all"]
    cache_kv = T["cache_kv"]

    state = {"bank": 0, "tmp": 0, "st": 0, "ub": 0}

    def bank():
        b = state["bank"]
        state["bank"] = (b + 1) % 8
        return banks[b], bank_res[b]

    def tmp():
        i = state["tmp"]
        state["tmp"] = (i + 1) % len(T["tmps"])
        return T["tmps"][i], T["tmp_res"][i]

    def stat():
        i = state["st"]
        state["st"] = (i + 1) % len(T["stats"])
        return T["stats"][i], T["stat_res"][i]

    def mm(out, lhsT, rhs, start, stop, reads, writes, skip=False):
        P.op("pe", lambda e: e.matmul(out, lhsT, rhs, start=start, stop=stop, skip_group_check=skip), reads, writes)

    def tr(out, in_, idn, reads, writes):
        P.op("pe", lambda e: e.transpose(out, in_, idn), reads, writes)

    def act(out, in_, func, reads, writes, scale=1.0, accum=None):
        if accum is None:
            P.op("act", lambda e: e.activation(out=out, in_=in_, func=func, scale=scale), reads, writes)
        else:
            P.op("act", lambda e: e.activation(out=out, in_=in_, func=func, scale=scale, accum_out=accum), reads, writes)

    def tt(eng, out, in0, in1, op, reads, writes):
        P.op(eng, lambda e: e.tensor_tensor(out=out, in0=in0, in1=in1, op=op), reads, writes)

    def ts(eng, out, in0, s1, s2, op0, op1, reads, writes):
        if s2 is None:
            P.op(eng, lambda e: e.tensor_scalar(out=out, in0=in0, scalar1=s1, scalar2=None, op0=op0), reads, writes)
        else:
            P.op(eng, lambda e: e.tensor_scalar(out=out, in0=in0, scalar1=s1, scalar2=s2, op0=op0, op1=op1), reads, writes)

    def stt(out, in0, scalar, in1, op0, op1, reads, writes):
        P.op("dve", lambda e: e.scalar_tensor_tensor(out=out, in0=in0, scalar=scalar, in1=in1, op0=op0, op1=op1), reads, writes)

    def cp(eng, out, in_, reads, writes):
        P.op(eng, lambda e: e.tensor_copy(out=out, in_=in_), reads, writes)

    def red(out, in_, reads, writes):
        P.op("dve", lambda e: e.tensor_reduce(out=out, in_=in_, axis=AX.X, op=ALU.add), reads, writes)

    def recip(out, in_, reads, writes):
        P.op("dve", lambda e: e.reciprocal(out=out, in_=in_), reads, writes)

    def dma(eng, out, in_, reads, writes):
        P.op(eng, lambda e: e.dma_start(out=out, in_=in_), reads, writes, dma=True)

    def rstd_of(ss_ap, n, eps, width, reads_res, rows=128):
        a, ar = stat()
        ts("dve", a[0:rows, 0:width], ss_ap, 1.0 / n, eps, ALU.mult, ALU.add, reads_res, [ar])
        b, br = stat()
        act(b[0:rows, 0:width], a[0:rows, 0:width], AF.Sqrt, [ar], [br])
        c, cr = stat()
        recip(c[0:rows, 0:width], b[0:rows, 0:width], [br], [cr])
        return c, cr

    hbm = T["hbm_res"]
    lamw = T["lamw"]
    lam_t = T["lam_t"]
    t0, t0r = tmp()
    tt("dve", t0[:, 0:128], lamw[:, 0, :], lamw[:, 1, :], ALU.mult, [CONST], [t0r])
    tt("dve", t0[:, 128:256], lamw[:, 2, :], lamw[:, 3, :], ALU.mult, [t0r], [t0r])
    red(lam_t[:, 0:2], t0[:, 0:256].rearrange("p (a b) -> p a b", a=2), [t0r], [T["lam_res"]])
    act(lam_t[:, 0:2], lam_t[:, 0:2], AF.Exp, [T["lam_res"]], [T["lam_res"]])
    tt("dve", lam_t[:, 2:3], lam_t[:, 0:1], lam_t[:, 1:2], ALU.subtract, [T["lam_res"]], [T["lam_res"]])
    ts("dve", lam_t[:, 3:4], lam_t[:, 2:3], LAM_INIT, None, ALU.add, None, [T["lam_res"]], [T["lam_res"]])
    lam_ap = lam_t[:, 3:4]
    LAMR = T["lam_res"]
    gs4 = T["gs4"]
    ts("dve", gs4[:, :], T["gsT4"][:, :], 1.0 - LAM_INIT, None, ALU.mult, None, [CONST], [T["gs4_res"]])
    SH, SHo, HC = T["SH"], T["SHo"], T["HC"]
    P.op("dve", lambda e: e.memset(HC[:, :, :], 0.0), [], [T["hc_res"]])
    P.op("dve", lambda e: e.memset(T["ones"][:, :], 1.0), [], [T["ones_res"]])
    cp("dve", T["identb"][:, :], ident[:, :], [CONST], [T["identb_res"]])
    for seq in range(4):
        po, por = T["po"], T["po_res"]
        dma("sp", po[0:15, :], T["state_pool"][seq, :, :], [], [por])
        bk, bkr = bank()
        for c in range(16):
            tr(bk[:, c * 15:(c + 1) * 15], po[0:15, c * 128:(c + 1) * 128], ident[0:15, 0:15], [por, CONST], [bkr])
        cp("dve", SH[:, seq, :, :], bk[:, 0:240].rearrange("p (c t) -> p c t", c=16), [bkr], [T["sh_res"]])

    def transpose_rows_to_fm(src, src_res, nblk, dst_fn, scale_fn, dst_res_fn, rows=128):
        for j0 in range(0, nblk, 4):
            n = min(4, nblk - j0)
            bk, bkr = bank()
            for j in range(n):
                tr(bk[:, j * 128:j * 128 + rows], src[0:rows, (j0 + j) * 128:(j0 + j + 1) * 128], ident[0:rows, 0:rows], [src_res, CONST], [bkr])
            pv = bk[:, 0:n * 128].rearrange("p (a b) -> p a b", a=n)[:, :, 0:rows]
            sc = scale_fn(j0, n)
            wr = dst_res_fn(j0, n)
            if isinstance(sc, float):
                ts("dve", dst_fn(j0, n), pv, sc, None, ALU.mult, None, [bkr], wr)
            else:
                tt("dve", dst_fn(j0, n), pv, sc, ALU.mult, [bkr, CONST], wr)

    def proj_fm(w, nkc, col0, nchunks, rhs_fn, rhs_res_fn, evac, krow0=0, hooks=None):
        kslab = min(nkc, 16)
        ncol = SLOT_E // kslab
        ncol = min(ncol, 512, nchunks * 128)
        cpb = ncol // 128
        nks = nkc // kslab
        for cg in range(nchunks // cpb):
            if hooks and cg in hooks and not P.dry:
                hooks[cg]()
            bks = [bank() for _ in range(cpb)]
            for ks in range(nks):
                slab, sres = ws.get(w, krow0 + ks * kslab, kslab, col0 + cg * ncol, ncol)
                if P.dry:
                    continue
                for m in range(cpb):
                    for kc in range(kslab):
                        kk = ks * kslab + kc
                        mm(bks[m][0][:, 0:G], slab[:, kc, m * 128:(m + 1) * 128], rhs_fn(kk),
                           kk == 0, kk == nkc - 1, [sres, rhs_res_fn(kk)], [bks[m][1]])
            if P.dry:
                continue
            for m in range(cpb):
                evac(cg * cpb + m, bks[m][0], bks[m][1])

    def proj_tm(w, nkc, col0, ncb, lhs_fn, lhs_res_fn, evac):
        kslab = 8
        nks = nkc // kslab
        for cb in range(ncb):
            bks = [bank() for _ in range(NSUB)]
            for ks in range(nks):
                slab, sres = ws.get(w, ks * kslab, kslab, col0 + cb * 512, 512)
                if P.dry:
                    continue
                for s in range(NSUB):
                    for kc in range(kslab):
                        kk = ks * kslab + kc
                        mm(bks[s][0][:, :], lhs_fn(kk, s), slab[:, kc, :], kk == 0, kk == nkc - 1,
                           [sres, lhs_res_fn(kk)], [bks[s][1]])
            if P.dry:
                continue
            for s in range(NSUB):
                evac(cb, s, bks[s][0], bks[s][1])

    xsq_res = T["xsq_res"]

    def norm_pre(src_rows_ap, src_res_fn, ss_known=None):
        xs = T["xs"][0]
        junk, jr = T["junk"], T["junk_res"]
        if ss_known is None:
            ssq, ssr = stat()
            for q in range(4):
                cs = slice(q * 1024, (q + 1) * 1024)
                dma("sp", xs[:, cs], src_rows_ap[:, cs], src_res_fn(q), [xsq_res[q]])
                act(junk[:, cs], xs[:, cs], AF.Square, [xsq_res[q]], [jr, ssr], accum=ssq[:, q:q + 1])
            tot, totr = stat()
            red(tot[:, 0:1], ssq[:, 0:4], [ssr], [totr])
        else:
            ssq, ssr = ss_known
            tot, totr = stat()
            red(tot[:, 0:1], ssq[:, 0:8], [ssr], [totr])
            for q in range(4):
                cs = slice(q * 1024, (q + 1) * 1024)
                dma("sp", xs[:, cs], src_rows_ap[:, cs], src_res_fn(q), [xsq_res[q]])
        rs, rsr = rstd_of(tot[:, 0:1], float(D), EPS, 1, [totr])
        for q in range(4):
            cs = slice(q * 1024, (q + 1) * 1024)
            ts("dve", xs[:, cs], xs[:, cs], rs[:, 0:1], None, ALU.mult, None, [xsq_res[q], rsr], [xsq_res[q]])

    def norm_post(s, gT):
        xs = T["xs"][0]
        for q in range(4):
            transpose_rows_to_fm(
                xs[:, q * 1024:(q + 1) * 1024], xsq_res[q], 8,
                lambda j0, n, q=q: B1[:, q * 8 + j0:q * 8 + j0 + n, s * 128:(s + 1) * 128],
                lambda j0, n, q=q: gT[:, q * 8 + j0:q * 8 + j0 + n].unsqueeze(2).to_broadcast([128, n, 128]),
                lambda j0, n, q=q: [b1_res[q * 8 + j] for j in range(j0, j0 + n)])

    identb = T["identb"]
    IDB = T["identb_res"]

    def attention_group(sts):
        osb, osr = T["osb"], T["osb_res"]
        OA, OAr = banks[0], bank_res[0]
        OB, OBr = banks[1], bank_res[1]
        SBk, SBr = banks[2], bank_res[2]
        its = []
        for s, st in enumerate(sts):
            if st < 16:
                segs = [(0, 128, [(kv_all[j * 128:(j + 1) * 128, :, :], 128, j == st, j) for j in range(st + 1)])]
            else:
                segs = []
                for i in range(2):
                    seq = (st - 16) * 2 + i
                    blks = [(cache_kv[seq, j * 128:(j + 1) * 128, :, :], 128, False, None) for j in range(16)]
                    r0 = st * 128 + i * 64
                    blks.append((kv_all[r0:r0 + 64, :, :], 64, False, st))
                    segs.append((i * 64, 64, blks))
            for (q0, Lq, blks) in segs:
                for hp in range(4):
                    for bi, (src, nk, diag, hst) in enumerate(blks):
                        its.append(dict(s=s, q0=q0, Lq=Lq, hp=hp, bi=bi, nb=len(blks), src=src, nk=nk, diag=diag, hst=hst))
        n = len(its)

        def LD(i):
            it = its[i]
            kv, kvr = T["kvb"][i % NKV], T["kvb_res"][i % NKV]
            hp, nk = it["hp"], it["nk"]
            rd = [hbm[("k", it["hst"], hp)], hbm[("v", it["hst"], hp)]] if it["hst"] is not None else []
            dma("pool", kv[0:nk, :, :], it["src"][:, :, hp * 512:(hp + 1) * 512], rd, [kvr])

        def TC(i):
            it = its[i]
            kv, kvr = T["kvb"][i % NKV], T["kvb_res"][i % NKV]
            kt, ktr = T["ktb"][i % 3], T["ktb_res"][i % 3]
            nk = it["nk"]
            tb, tbr = banks[3 + i % 2], bank_res[3 + i % 2]
            tbv = tb[:, :].bitcast(BF16)
            for u in range(4):
                tr(tbv[:, u * 128:u * 128 + nk], kv[0:nk, 0, u * 128:(u + 1) * 128], identb[0:nk, 0:nk], [kvr, IDB], [tbr])
            cp("dve", kt[:, :, 0:nk], tbv[:, 0:512].rearrange("p (a b) -> p a b", a=4)[:, :, 0:nk], [tbr], [ktr])

        def LE(i):
            it = its[i]
            kt, ktr = T["ktb"][i % 3], T["ktb_res"][i % 3]
            pt, ptr_ = T["ptb"][i % 3], T["ptb_res"][i % 3]
            nk, Lq, hp = it["nk"], it["Lq"], it["hp"]
            qc0 = it["s"] * 128 + it["q0"]
            lg, lgr = banks[5 + i % 2], bank_res[5 + i % 2]
            for u in range(4):
                mm(lg[0:nk, u * 128:u * 128 + Lq], kt[:, u, 0:nk], BIG[:, QT0 + hp * 4 + u, qc0:qc0 + Lq],
                   True, True, [ktr, big_res[QT0 + hp * 4 + u]], [lgr])
            act(pt[0:nk, :, 0:Lq], lg[0:nk, :].rearrange("p (a b) -> p a b", a=4)[:, :, 0:Lq], AF.Exp, [lgr], [ptr_])
            if it["diag"]:
                P.op("dve", lambda e, a=pt[64:128, :, 0:64]: e.memset(a, 0.0), [], [ptr_])

        def PV(i):
            it = its[i]
            kv, kvr = T["kvb"][i % NKV], T["kvb_res"][i % NKV]
            pt, ptr_ = T["ptb"][i % 3], T["ptb_res"][i % 3]
            nk, Lq, hp, bi, nb = it["nk"], it["Lq"], it["hp"], it["bi"], it["nb"]
            for u in range(4):
                ob, obr = (OA, OAr) if u < 2 else (OB, OBr)
                mm(ob[0:Lq, (u % 2) * 256:(u % 2 + 1) * 256], pt[0:nk, u, 0:Lq], kv[0:nk, 1, (u // 2) * 256:(u // 2 + 1) * 256],
                   bi == 0 and u % 2 == 0, bi == nb - 1, [ptr_, kvr], [obr], skip=True)
            for u in range(4):
                mm(SBk[0:Lq, u:u + 1], pt[0:nk, u, 0:Lq], T["ones"][0:nk, 0:1],
                   bi == 0 and u == 0, bi == nb - 1, [ptr_, T["ones_res"]], [SBr], skip=True)
            if bi != nb - 1:
                return
            rsm, rsmr = stat()
            recip(rsm[0:Lq, 0:4], SBk[0:Lq, 0:4], [SBr], [rsmr])
            cl, clr = stat()
            ts("dve", cl[0:Lq, 0:4], rsm[0:Lq, 0:4], lam_ap[0:Lq, :], None, ALU.mult, None, [rsmr, LAMR], [clr])
            for j in range(2):
                ob, obr = (OA, OAr) if j == 0 else (OB, OBr)
                t1, t1r = tmp()
                act(t1[0:Lq, 0:256], ob[0:Lq, 256:512], AF.Copy, [obr, clr], [t1r], scale=cl[0:Lq, 2 * j + 1:2 * j + 2])
                h = hp * 2 + j
                stt(osb[0:Lq, h * 256:(h + 1) * 256], ob[0:Lq, 0:256], rsm[0:Lq, 2 * j:2 * j + 1],
                    t1[0:Lq, 0:256], ALU.mult, ALU.subtract, [obr, rsmr, t1r], [osr])
            if hp != 3:
                return
            qc0 = it["s"] * 128 + it["q0"]
            jk = B1[:, 0:8, :].rearrange("p a b -> p (a b)")
            jkr = [b1_res[j] for j in range(8)]
            act(jk[0:Lq, 0:2048], osb[0:Lq, :], AF.Square, [osr], jkr)
            ss8, ss8r = stat()
            red(ss8[0:Lq, 0:8], jk[0:Lq, 0:2048].rearrange("p (h e) -> p h e", h=8), jkr, [ss8r])
            rs8, rs8r = rstd_of(ss8[0:Lq, 0:8], 256.0, SUB_EPS, 8, [ss8r], rows=Lq)
            tt("dve", osb[0:Lq, :].rearrange("p (h e) -> p h e", h=8), osb[0:Lq, :].rearrange("p (h e) -> p h e", h=8),
               rs8[0:Lq, 0:8].unsqueeze(2).to_broadcast([Lq, 8, 256]), ALU.mult, [osr, rs8r], [osr])
            transpose_rows_to_fm(
                osb, osr, 16,
                lambda j0, n: BIG[:, OT0 + j0:OT0 + j0 + n, qc0:qc0 + Lq],
                lambda j0, n: gs4[:, 0:n].unsqueeze(2).to_broadcast([128, n, Lq]),
                lambda j0, n: [big_res[OT0 + j] for j in range(j0, j0 + n)], rows=Lq)

        LDA = NKV - 1
        for i in range(min(LDA, n)):
            LD(i)
        for i in range(min(2, n)):
            TC(i)
        if n > 0:
            LE(0)
        for t in range(n):
            if t + LDA < n:
                LD(t + LDA)
            if t + 2 < n:
                TC(t + 2)
            if t + 1 < n:
                LE(t + 1)
            PV(t)

    ngroups = NST // NSUB
    for g in range(ngroups):
        sts = [g * NSUB + s for s in range(NSUB)]
        is_sample = sts[0] >= 16
        if P.dry:
            pass
        def a1_pre(st):
            norm_pre(x_all[st * 128:(st + 1) * 128, :], lambda q: [])

        def a1_rest(sts_n, first_pre_done):
            for s_, st_ in enumerate(sts_n):
                if not (s_ == 0 and first_pre_done):
                    a1_pre(st_)
                norm_post(s_, T["g1T"])
        if not P.dry and g == 0:
            a1_rest(sts, False)

        def evac_qkv(cb, s, bk, bkr):
            st = sts[s]
            rows = slice(st * 128, (st + 1) * 128)
            if cb < 8:
                sq, sqr = tmp()
                act(sq[:, :], bk[:, :], AF.Square, [bkr], [sqr])
                ss4, ss4r = stat()
                red(ss4[:, 0:4], sq[:, :].rearrange("p (u d) -> p u d", u=4), [sqr], [ss4r])
                rs4, rs4r = rstd_of(ss4[:, 0:4], 128.0, EPS, 4, [ss4r])
                t1, t1r = tmp()
                tt("dve", t1[:, :].rearrange("p (u d) -> p u d", u=4), bk[:, :].rearrange("p (u d) -> p u d", u=4),
                   rs4[:, 0:4].unsqueeze(2).to_broadcast([128, 4, 128]), ALU.mult, [bkr, rs4r], [t1r])
                t2, t2r = tmp()
                gb_ = T["gq_b"] if cb < 4 else T["gk_b"]
                tt("dve", t2[:, :], t1[:, :], gb_[:, :], ALU.mult, [t1r, CONST], [t2r])
                if cb < 4:
                    transpose_rows_to_fm(
                        t2, t2r, 4,
                        lambda j0, n: BIG[:, QT0 + cb * 4 + j0:QT0 + cb * 4 + j0 + n, s * 128:(s + 1) * 128],
                        lambda j0, n: float(128 ** -0.5),
                        lambda j0, n: [big_res[QT0 + cb * 4 + j] for j in range(j0, j0 + n)])
                else:
                    hp = cb - 4
                    dma("sp", kv_all[rows, 0, hp * 512:(hp + 1) * 512], t2[:, :], [t2r], [hbm[("k", st, hp)]])
            else:
                hp = cb - 8
                t1, t1r = tmp()
                act(t1[:, :], bk[:, :], AF.Copy, [bkr], [t1r])
                dma("sp", kv_all[rows, 1, hp * 512:(hp + 1) * 512], t1[:, :], [t1r], [hbm[("v", st, hp)]])

        proj_tm(T["w_in"], 32, 0, 12, lambda kk, s: B1[:, kk, s * 128:(s + 1) * 128], lambda kk: b1_res[kk], evac_qkv)

        if is_sample:
            nseg, L = 4, 64
        else:
            nseg, L = 1, G
        W_ = 15 + L

        def evac_ugg(chunk, bk, bkr):
            if chunk < 16:
                c = chunk
                gi = c // 4
                w = WINS[gi]
                i = state["ub"]
                state["ub"] = (i + 1) % 2
                ub, ubr = T["ub"][i], T["ub_res"][i]
                sa, sar = T["sa"][i], T["sa_res"][i]
                sb_, sbr = T["sb"][i], T["sb_res"][i]
                ubv = ub[:, 0:nseg * W_].rearrange("p (a b) -> p a b", a=nseg)
                sav = sa[:, 0:nseg * W_].rearrange("p (a b) -> p a b", a=nseg)
                sbv = sb_[:, 0:nseg * W_].rearrange("p (a b) -> p a b", a=nseg)
                if is_sample:
                    cp("dve", ubv[:, :, 0:15], SH[:, :, c, :], [T["sh_res"]], [ubr])
                else:
                    cp("dve", ubv[:, :, 0:15], HC[:, c:c + 1, :], [T["hc_res"]], [ubr])
                act(ubv[:, :, 15:W_], bk[:, 0:G].rearrange("p (a b) -> p a b", a=nseg), AF.Copy, [bkr], [ubr])
                cur, curr = ubv, ubr
                outs = [(sav, sar), (sbv, sbr)]
                for lev in range(gi + 1):
                    sh = 1 << lev
                    lo = (1 << (lev + 1)) - 1
                    nxt, nxtr = outs[lev % 2]
                    tt("dve", nxt[:, :, lo:W_], cur[:, :, lo:W_], cur[:, :, lo - sh:W_ - sh], ALU.add, [curr], [nxtr])
                    cur, curr = nxt, nxtr
                pl = BIG[:, PLD0 + c, :].rearrange("p (a b) -> p a b", a=nseg)
                stt(pl, cur[:, :, 15:W_], 1.0 / w, ubv[:, :, 15:W_], ALU.mult, ALU.subtract, [curr, ubr], [big_res[PLD0 + c]])
                if (not is_sample) and g == 0:
                    t1, t1r = stat()
                    tt("dve", t1[:, 0:15], cur[:, 0, 15:30], T["invc"][:, gi, 0:15], ALU.mult, [curr, CONST], [t1r])
                    tt("dve", BIG[:, PLD0 + c, 0:15], t1[:, 0:15], ubv[:, 0, 15:30], ALU.subtract, [t1r, ubr], [big_res[PLD0 + c]])
                if is_sample:
                    cp("dve", SHo[:, :, c, :], ubv[:, :, L:L + 15], [ubr], [T["sho_res"]])
                else:
                    cp("dve", HC[:, c:c + 1, :], ubv[:, :, L:L + 15], [ubr], [T["hc_res"]])
            elif chunk < 48:
                c = chunk - 16
                act(BIG[:, SGA0 + c, :], bk[:, 0:G], AF.Sigmoid, [bkr], [big_res[SGA0 + c]])
            else:
                c = chunk - 48
                act(BIG[:, SGB0 + c, :], bk[:, 0:G], AF.Sigmoid, [bkr], [big_res[SGB0 + c]])

        proj_fm(T["w_in"], 32, 6144, 80, lambda kk: B1[:, kk, :], lambda kk: b1_res[kk], evac_ugg)

        if not P.dry:
            def pool_out(src3, src_res, seq_out):
                po, por = T["po"], T["po_res"]
                for c4 in range(4):
                    bk, bkr = bank()
                    for j in range(4):
                        c = c4 * 4 + j
                        tr(bk[0:15, j * 128:(j + 1) * 128], src3(c), ident[:, :], [src_res, CONST], [bkr])
                    cp("dve", po[0:15, c4 * 512:(c4 + 1) * 512], bk[0:15, :], [bkr], [por])
                dma("sp", pool_all[seq_out, :, :], po[0:15, :], [por], [])
            if sts[-1] == 15:
                pool_out(lambda c: HC[:, c, :], T["hc_res"], 0)
            if is_sample:
                for seq in range(4):
                    pool_out(lambda c, seq=seq: SHo[:, seq, c, :], T["sho_res"], 1 + seq)

        for gi in range(4):
            def evac_mix(m, bk, bkr, gi=gi):
                c = gi * 4 + m
                ts("dve", BIG[:, MIX0 + c, :], bk[:, 0:G], T["psT"][:, c:c + 1], None, ALU.mult, None, [bkr, CONST], [big_res[MIX0 + c]])
            proj_fm(T["w_pool"], 4, 0, 4, lambda kk, gi=gi: BIG[:, PLD0 + gi * 4 + kk, :], lambda kk, gi=gi: big_res[PLD0 + gi * 4 + kk],
                    evac_mix, krow0=gi * 4)

        def evac_yb(c, bk, bkr):
            tt("dve", BIG[:, SGB0 + c, :], bk[:, 0:G], BIG[:, SGB0 + c, :], ALU.mult, [bkr, big_res[SGB0 + c]], [big_res[SGB0 + c]])
        proj_fm(T["w_pool_out"], 16, 0, 32, lambda kk: BIG[:, MIX0 + kk, :], lambda kk: big_res[MIX0 + kk], evac_yb)

        if not P.dry:
            attention_group(sts)

        def evac_ya(c, bk, bkr):
            t1, t1r = tmp()
            tt("dve", t1[:, 0:G], bk[:, 0:G], BIG[:, SGA0 + c, :], ALU.mult, [bkr, big_res[SGA0 + c]], [t1r])
            tt("dve", B1[:, c, :], t1[:, 0:G], BIG[:, SGB0 + c, :], ALU.add, [t1r, big_res[SGB0 + c]], [b1_res[c]])
        proj_fm(T["w_attn_out"], 16, 0, 32, lambda kk: BIG[:, OT0 + kk, :], lambda kk: big_res[OT0 + kk], evac_ya)

        def evac_wo(cb, s, bk, bkr):
            st = sts[s]
            rows = slice(st * 128, (st + 1) * 128)
            xb, xbr = tmp()
            dma("sp", xb[:, :], x_all[rows, cb * 512:(cb + 1) * 512], [], [xbr])
            t1, t1r = tmp()
            tt("dve", t1[:, :], bk[:, :], xb[:, :], ALU.add, [bkr, xbr], [t1r])
            act(xb[:, :], t1[:, :], AF.Square, [t1r], [xbr, T["ssx_res"][s]], accum=T["ssx"][:, s * 8 + cb:s * 8 + cb + 1])
            dma("sp", y_all[rows, cb * 512:(cb + 1) * 512], t1[:, :], [t1r], [hbm[("y", st, cb)]])
        proj_tm(T["w_o"], 32, 0, 8, lambda kk, s: B1[:, kk, s * 128:(s + 1) * 128], lambda kk: b1_res[kk], evac_wo)

        if not P.dry:
            for s, st in enumerate(sts):
                norm_pre(y_all[st * 128:(st + 1) * 128, :], lambda q, st=st: [hbm[("y", st, 2 * q)], hbm[("y", st, 2 * q + 1)]],
                         ss_known=(T["ssx"][:, s * 8:(s + 1) * 8], T["ssx_res"][s]))
                norm_post(s, T["g2T"])

        def evac_up(c, bk, bkr):
            t1, t1r = tmp()
            act(t1[:, 0:G], bk[:, 0:G], AF.Relu, [bkr], [t1r])
            tt("dve", BIG[:, c, :], t1[:, 0:G], t1[:, 0:G], ALU.mult, [t1r], [big_res[c]])
        nxt = [(g + 1) * NSUB + s_ for s_ in range(NSUB)] if g + 1 < ngroups else None
        proj_fm(T["w_up"], 32, 0, 128, lambda kk: B1[:, kk, :], lambda kk: b1_res[kk], evac_up,
                hooks=({32: (lambda: a1_pre(nxt[0]))} if nxt else None))
        if nxt and not P.dry:
            a1_rest(nxt, True)

        def evac_down(cb, s, bk, bkr):
            st = sts[s]
            rows = slice(st * 128, (st + 1) * 128)
            xb, xbr = tmp()
            dma("sp", xb[:, :], y_all[rows, cb * 512:(cb + 1) * 512], [hbm[("y", st, cb)]], [xbr])
            t1, t1r = tmp()
            tt("dve", t1[:, :], bk[:, :], xb[:, :], ALU.add, [bkr, xbr], [t1r])
            dma("sp", y_all[rows, cb * 512:(cb + 1) * 512], t1[:, :], [t1r], [hbm[("y", st, cb)]])
        proj_tm(T["w_down"], 128, 0, 8, lambda kk, s: BIG[:, kk, s * 128:(s + 1) * 128], lambda kk: big_res[kk], evac_down)


def build_nc():
    nc = bass.Bass("TRN2", target_bir_lowering=False)
    T = {}

    def din(name, shape):
        return nc.dram_tensor(name, list(shape), F32, kind="ExternalInput").ap()

    def dout(name, shape):
        return nc.dram_tensor(name, list(shape), F32, kind="ExternalOutput").ap()

    T["x_all"] = din("x_all", (TOK, D))
    T["cache_kv"] = din("cache_kv", (4, 2048, 2, 2048))
    T["state_pool"] = din("state_pool", (4, 15, 2048))
    T["w_in"] = din("w_in", (4096, 16384))
    T["w_attn_out"] = din("w_attn_out", (2048, 4096))
    T["w_pool"] = din("w_pool", (2048, 512))
    T["w_pool_out"] = din("w_pool_out", (2048, 4096))
    T["w_o"] = din("w_o", (4096, 4096))
    T["w_up"] = din("w_up", (4096, 16384))
    T["w_down"] = din("w_down", (16384, 4096))
    d_g1T = din("g1T", (128, 32))
    d_g2T = din("g2T", (128, 32))
    d_psT = din("psT", (128, 16))
    d_gsT4 = din("gsT4", (128, 4))
    d_gq = din("gq_b", (128, 512))
    d_gk = din("gk_b", (128, 512))
    d_lam = din("lamw", (128, 512))
    d_ident = din("ident", (128, 128))
    d_invc = din("invc", (128, 64))
    T["y_all"] = dout("y_all", (TOK, D))
    T["kv_all"] = dout("kv_all", (TOK, 2, 2048))
    T["pool_all"] = dout("pool_all", (5, 15, 2048))

    with ExitStack() as es:
        def sb(name, shape, dt):
            return es.enter_context(nc.sbuf_tensor("sb_" + name, list(shape), dt))

        slots = [sb(f"slot{i}", (128, SLOT_E), BF16) for i in range(NSLOT)]
        slot_res = [Res(f"slot{i}") for i in range(NSLOT)]
        T["B1"] = sb("B1", (128, 32, G), BF16)
        T["b1_res"] = [Res(f"b1_{i}") for i in range(32)]
        T["BIG"] = sb("BIG", (128, 128, G), BF16)
        T["big_res"] = [Res(f"big_{i}") for i in range(128)]
        T["xs"] = [sb("xs0", (128, D), F32)]
        T["osb"] = sb("osb", (128, 2048), F32)
        T["osb_res"] = Res("osb")
        T["junk"] = T["osb"][:, :].bitcast(BF16)
        T["junk_res"] = T["osb_res"]
        T["xsq_res"] = [Res(f"xsq{i}") for i in range(4)]
        T["ssx"] = sb("ssx", (128, NSUB * 8), F32)
        T["ssx_res"] = [Res(f"ssx{i}") for i in range(NSUB)]
        T["po"] = T["osb"]
        T["po_res"] = T["osb_res"]
        NT = 6
        T["tmps"] = [sb(f"tmp{i}", (128, 512), F32) for i in range(NT)]
        T["tmp_res"] = [Res(f"tmp{i}") for i in range(NT)]
        NS = 16
        T["stats"] = [sb(f"stat{i}", (128, 16), F32) for i in range(NS)]
        T["stat_res"] = [Res(f"stat{i}") for i in range(NS)]
        T["kvb"] = [sb(f"kvb{i}", (128, 2, 512), BF16) for i in range(NKV)]
        T["kvb_res"] = [Res(f"kvb{i}") for i in range(NKV)]
        T["ktb"] = [sb(f"ktb{i}", (128, 4, 128), BF16) for i in range(3)]
        T["ktb_res"] = [Res(f"ktb{i}") for i in range(3)]
        T["ptb"] = [sb(f"ptb{i}", (128, 4, 128), BF16) for i in range(3)]
        T["ptb_res"] = [Res(f"ptb{i}") for i in range(3)]
        T["identb"] = sb("identb", (128, 128), BF16)
        T["identb_res"] = Res("identb", const=True)
        UBW = 320
        for nm in ("ub", "sa", "sb"):
            T[nm] = [sb(f"{nm}{i}", (128, UBW), F32) for i in range(2)]
            T[nm + "_res"] = [Res(f"{nm}{i}") for i in range(2)]
        T["SH"] = sb("SH", (128, 4, 16, 15), F32)
        T["sh_res"] = Res("sh")
        T["SHo"] = T["SH"]
        T["sho_res"] = T["sh_res"]
        T["HC"] = sb("HC", (128, 16, 15), F32)
        T["hc_res"] = Res("hc")
        T["ones"] = sb("ones", (128, 2), BF16)
        T["ones_res"] = Res("ones")
        T["lam_t"] = sb("lam_t", (128, 4), F32)
        T["lam_res"] = Res("lam")
        T["gs4"] = sb("gs4", (128, 4), F32)
        T["gs4_res"] = Res("gs4")
        T["g1T"] = sb("g1T", (128, 32), F32)
        T["g2T"] = sb("g2T", (128, 32), F32)
        T["psT"] = sb("psT", (128, 16), F32)
        T["gsT4"] = sb("gsT4", (128, 4), F32)
        T["gq_b"] = sb("gq_b", (128, 512), F32)
        T["gk_b"] = sb("gk_b", (128, 512), F32)
        T["lamw"] = sb("lamw", (128, 4, 128), F32)
        T["ident"] = sb("ident", (128, 128), F32)
        T["invc"] = sb("invc", (128, 4, 16), F32)
        T["const_res"] = Res("const", const=True)
        banks = [es.enter_context(nc.psum_tensor(f"bank{i}", [128, 512], F32)) for i in range(8)]
        T["banks"] = banks
        T["bank_res"] = [Res(f"bank{i}") for i in range(8)]
        hbm = {}
        for st in range(NST):
            for hp in range(4):
                hbm[("k", st, hp)] = Res(f"k{st}_{hp}")
                hbm[("v", st, hp)] = Res(f"v{st}_{hp}")
            for cb in range(8):
                hbm[("y", st, cb)] = Res(f"y{st}_{cb}")
        T["hbm_res"] = hbm

        Pd = Prog()
        Pd.dry = True
        wsd = WStream(Pd, slots, slot_res, None)
        build_core_program(nc, T, Pd, wsd)
        sched = wsd.rec

        P = Prog()
        CONST = T["const_res"]
        cl = [(T["g1T"][:, :], d_g1T), (T["g2T"][:, :], d_g2T), (T["psT"][:, :], d_psT), (T["gsT4"][:, :], d_gsT4),
              (T["gq_b"][:, :], d_gq), (T["gk_b"][:, :], d_gk), (T["lamw"][:, :, :], d_lam.rearrange("p (a b) -> p a b", a=4)),
              (T["ident"][:, :], d_ident), (T["invc"][:, :, :], d_invc.rearrange("p (a b) -> p a b", a=4))]
        cres = [Res(f"c{i}") for i in range(len(cl))]
        for (dst, src), r in zip(cl, cres):
            P.op("sp", lambda e, d=dst, s=src: e.dma_start(out=d, in_=s), [], [r], dma=True)
        P.op("dve", lambda e: e.memset(T["gs4"][:, :], 0.0), cres, [CONST, T["gs4_res"]])
        npass = len(sched) // (NST // NSUB)
        scratch = [nc.dram_tensor(f"wscr{i}", [240 * 128, SLOT_E], BF16, kind="Internal").ap()
                   for i in range((npass + 239) // 240)]
        ws = WStream(P, slots, slot_res, sched, scratch, npass)
        build_core_program(nc, T, P, ws)
        P.finish()
        import os
        if os.environ.get("KDEBUG"):
            print("ops", {e: len(P.ops[e]) for e in P.ENG}, "slabs", len(sched), "sbuf left", nc.sbuf_bytes_remaining, flush=True)

        sem_names = {e: es.enter_context(nc.semaphore(f"s_{e}")) for e in Prog.ENG}
        dma_sems = {}
        for e in ("sp", "pool"):
            for i in range(P.ndma):
                dma_sems[(e, i)] = es.enter_context(nc.semaphore(f"d_{e}{i}"))
        with nc.Block() as block:
            P.emit(nc, block, sem_names, dma_sems)
    return nc


_NC_CACHE = {}


def _w_pool_rows(w_pool):
    return np.ascontiguousarray(w_pool.reshape(2048, 512))


def kernel(x_prompt, x_sample, cache_k, cache_v, state_pool, g_norm1, w_in, g_q, g_k, lambda_q1,
           lambda_k1, lambda_q2, lambda_k2, g_subln, w_attn_out, w_pool, pool_scale, w_pool_out, w_o,
           g_norm2, w_up, w_down):
    f = np.float32
    x_prompt = np.asarray(x_prompt, f)
    x_sample = np.asarray(x_sample, f)
    cache_k = np.asarray(cache_k, f)
    cache_v = np.asarray(cache_v, f)
    state_pool = np.asarray(state_pool, f)
    if "nc" not in _NC_CACHE:
        _NC_CACHE["nc"] = build_nc()
    nc = _NC_CACHE["nc"]

    def fmT(v, n):
        return np.ascontiguousarray(np.asarray(v, f).reshape(n, 128).T)

    g1T = fmT(g_norm1[0], 32)
    g2T = fmT(g_norm2[0], 32)
    psT = fmT(pool_scale[0], 16)
    gs = np.asarray(g_subln[0], f).reshape(2, 128).T
    gsT4 = np.ascontiguousarray(np.concatenate([gs, gs], axis=1))
    gq_b = np.ascontiguousarray(np.broadcast_to(np.tile(np.asarray(g_q[0], f), 4)[None, :], (128, 512)))
    gk_b = np.ascontiguousarray(np.broadcast_to(np.tile(np.asarray(g_k[0], f), 4)[None, :], (128, 512)))
    lamw = np.concatenate([np.asarray(a[0], f) for a in (lambda_q1, lambda_k1, lambda_q2, lambda_k2)])
    lamw = np.ascontiguousarray(np.broadcast_to(lamw[None, :], (128, 512)))
    ident = np.eye(128, dtype=f)
    invc = np.zeros((4, 16), f)
    for gi, w in enumerate(WINS):
        for t in range(16):
            invc[gi, t] = 1.0 / min(t + 1, w)
    invc = np.ascontiguousarray(np.broadcast_to(invc.reshape(1, 64), (128, 64)))
    shared = {
        "w_in": np.asarray(w_in[0], f), "w_attn_out": np.asarray(w_attn_out[0], f),
        "w_pool": _w_pool_rows(np.asarray(w_pool[0], f)), "w_pool_out": np.asarray(w_pool_out[0], f),
        "w_o": np.asarray(w_o[0], f), "w_up": np.asarray(w_up[0], f), "w_down": np.asarray(w_down[0], f),
        "g1T": g1T, "g2T": g2T, "psT": psT, "gsT4": gsT4, "gq_b": gq_b, "gk_b": gk_b, "lamw": lamw,
        "ident": ident, "invc": invc,
    }
    in_maps = []
    for c in range(8):
        m = dict(shared)
        m["x_all"] = np.ascontiguousarray(np.concatenate([x_prompt[c], x_sample[4 * c:4 * c + 4].reshape(256, D)], axis=0))
        m["cache_kv"] = np.ascontiguousarray(np.stack(
            [cache_k[0, 4 * c:4 * c + 4].reshape(4, 2048, 2048), cache_v[0, 4 * c:4 * c + 4].reshape(4, 2048, 2048)], axis=2))
        m["state_pool"] = np.ascontiguousarray(state_pool[0, 4 * c:4 * c + 4])
        in_maps.append(m)
    res = run_bass_kernel_spmd(nc, in_maps, core_ids=list(range(8)))
    R = res.results
    y_p = np.stack([R[c]["y_all"][:2048] for c in range(8)])
    y_s = np.concatenate([R[c]["y_all"][2048:].reshape(4, 64, D) for c in range(8)])
    k_p = np.stack([R[c]["kv_all"][:2048, 0] for c in range(8)]).reshape(1, 8, 2048, 8, 2, 128)
    v_p = np.stack([R[c]["kv_all"][:2048, 1] for c in range(8)]).reshape(1, 8, 2048, 8, 256)
    k_s = np.concatenate([R[c]["kv_all"][2048:, 0].reshape(4, 64, 2048) for c in range(8)]).reshape(1, 32, 64, 8, 2, 128)
    v_s = np.concatenate([R[c]["kv_all"][2048:, 1].reshape(4, 64, 2048) for c in range(8)]).reshape(1, 32, 64, 8, 256)
    p_p = np.stack([R[c]["pool_all"][0] for c in range(8)]).reshape(1, 8, 15, 2048)
    p_s = np.concatenate([R[c]["pool_all"][1:5] for c in range(8)]).reshape(1, 32, 15, 2048)
    return (y_p.astype(f), y_s.astype(f), k_p.astype(f), v_p.astype(f), p_p.astype(f),
            k_s.astype(f), v_s.astype(f), p_s.astype(f))
```

```python
import numpy as np
import concourse.bass as bass
import concourse.mybir as mybir
from concourse.bass_utils import run_bass_kernel_spmd
from contextlib import ExitStack

F32 = mybir.dt.float32
BF16 = mybir.dt.bfloat16
AF = mybir.ActivationFunctionType
ALU = mybir.AluOpType
AX = mybir.AxisListType

D = 4096
NST = 18
NSUB = 2
G = NSUB * 128
TOK = NST * 128
NSLOT = 6
NKV = 8
SLOT_E = 4096
LAM_INIT = 0.8 - 0.6 * 1.0
EPS = 1e-6
SUB_EPS = 1e-5
WINS = (2, 4, 8, 16)


class Res:
    __slots__ = ("name", "lastw", "readers", "const")

    def __init__(self, name, const=False):
        self.name = name
        self.lastw = None
        self.readers = []
        self.const = const


class Prog:
    ENG = ("pe", "act", "dve", "pool", "sp")

    def __init__(self, ndma=8):
        self.ops = {e: [] for e in self.ENG}
        self.seen = {e: {} for e in self.ENG}
        self.dma_n = {e: 0 for e in self.ENG}
        self.dma_last = {}
        self.ndma = ndma
        self.dry = False

    def op(self, eng, fn, reads=(), writes=(), dma=False):
        if self.dry:
            return None
        deps = []
        for r in reads:
            if r.lastw is not None:
                deps.append(r.lastw)
        for w in writes:
            if w.lastw is not None:
                deps.append(w.lastw)
            deps.extend(w.readers)
        idx = len(self.ops[eng])
        if dma:
            slot = self.dma_n[eng] % self.ndma
            use = self.dma_n[eng] // self.ndma + 1
            self.dma_n[eng] += 1
            prev = self.dma_last.get((eng, slot))
            if prev is not None:
                deps.append(prev)
            ev = ("dma", (eng, slot), 16 * use)
            self.dma_last[(eng, slot)] = ev
        else:
            ev = ("eng", eng, idx)
        best = {}
        for d in deps:
            kind, key, val = d
            if kind == "eng" and key == eng and eng == "pe":
                continue
            k = (kind, key)
            if best.get(k, -1) < val:
                best[k] = val
        waits = []
        seen = self.seen[eng]
        for k, val in best.items():
            if seen.get(k, -1) >= val:
                continue
            seen[k] = val
            waits.append((k[0], k[1], val))
            if k[0] == "eng":
                self.ops[k[1]][val]["inc"] = True
        self.ops[eng].append({"fn": fn, "waits": waits, "inc": False, "dma": ev if dma else None})
        for r in reads:
            if not r.const:
                r.readers.append(ev)
        for w in writes:
            w.lastw = ev
            w.readers = []
        return ev

    def finish(self):
        deps = list(self.dma_last.values())
        best = {}
        for kind, key, val in deps:
            best[(kind, key)] = max(best.get((kind, key), -1), val)
        waits = [(k[0], k[1], v) for k, v in best.items()]
        for e in self.ENG:
            if e != "sp" and self.ops[e]:
                i = len(self.ops[e]) - 1
                if self.ops[e][i]["dma"] is None:
                    self.ops[e][i]["inc"] = True
                    waits.append(("eng", e, i))
        self.ops["sp"].append({"fn": None, "waits": waits, "inc": False, "dma": None})

    def emit(self, nc, block, sems, dma_sems):
        for e in self.ENG:
            c = 0
            for o in self.ops[e]:
                if o["inc"]:
                    c += 1
                    o["val"] = c
        ops = self.ops

        def run(e, engine):
            for o in ops[e]:
                for kind, key, val in o["waits"]:
                    if kind == "eng":
                        engine.wait_ge(sems[key], ops[key][val]["val"])
                    else:
                        engine.wait_ge(dma_sems[key], val)
                if o["fn"] is None:
                    continue
                ins = o["fn"](engine)
                if o["dma"] is not None:
                    ins.then_inc(dma_sems[o["dma"][1]], 16)
                elif o["inc"]:
                    ins.then_inc(sems[e], 1)

        @block.tensor
        def _(t):
            run("pe", t)

        @block.scalar
        def _(s):
            run("act", s)

        @block.vector
        def _(v):
            run("dve", v)

        @block.gpsimd
        def _(g):
            run("pool", g)

        @block.sync
        def _(sy):
            run("sp", sy)


class WStream:
    def __init__(self, P, slots, slot_res, sched=None, scratch=None, npass=None):
        self.P = P
        self.slots = slots
        self.res = slot_res
        self.sched = sched
        self.rec = []
        self.i = 0
        self.issued = 0
        self.scratch = scratch
        self.npass = npass
        self.scr_res = [Res(f"scr{i}") for i in range(npass)] if npass else None

    def _issue(self, j):
        w, r0, nk, c0, ncol = self.sched[j]
        s = j % NSLOT
        if self.scratch is None or j < self.npass:
            dst = self.slots[s][:, 0:nk * ncol].rearrange("p (k n) -> p k n", k=nk)
            src = w[r0 * 128:(r0 + nk) * 128, c0:c0 + ncol].rearrange("(k p) n -> p k n", p=128)
            self.P.op("pool", lambda e, d=dst, s_=src: e.dma_start(out=d, in_=s_), reads=(), writes=(self.res[s],), dma=True)
        else:
            jj = j % self.npass
            assert self.sched[jj][1:] == self.sched[j][1:]
            sd = self.scratch[jj // 240][(jj % 240) * 128:(jj % 240 + 1) * 128, 0:nk * ncol]
            ds = self.slots[s][:, 0:nk * ncol]
            self.P.op("pool", lambda e, d=ds, s_=sd: e.dma_start(out=d, in_=s_), reads=(self.scr_res[jj],), writes=(self.res[s],), dma=True)

    def get(self, w, r0, nk, c0, ncol):
        desc = (w, r0, nk, c0, ncol)
        if self.sched is None:
            self.rec.append(desc)
            self.i += 1
            return None, None
        j = self.i
        d = self.sched[j]
        assert d[1:] == desc[1:], (d[1:], desc[1:])
        while self.issued < len(self.sched) and self.issued < j + NSLOT:
            self._issue(self.issued)
            self.issued += 1
        self.i += 1
        s = j % NSLOT
        if self.scratch is not None and j < self.npass:
            sd = self.scratch[j // 240][(j % 240) * 128:(j % 240 + 1) * 128, 0:nk * ncol]
            ss = self.slots[s][:, 0:nk * ncol]
            self.P.op("sp", lambda e, d=sd, s_=ss: e.dma_start(out=d, in_=s_), reads=(self.res[s],), writes=(self.scr_res[j],), dma=True)
        view = self.slots[s][:, 0:nk * ncol].rearrange("p (k n) -> p k n", k=nk)
        return view, self.res[s]


def build_core_program(nc, T, P, ws):
    ident = T["ident"]
    CONST = T["const_res"]
    banks = T["banks"]
    bank_res = T["bank_res"]
    B1, b1_res = T["B1"], T["b1_res"]
    BIG, big_res = T["BIG"], T["big_res"]
    SGA0, SGB0, OT0, MIX0, PLD0, QT0 = 0, 32, 64, 80, 96, 112
    x_all, y_all, kv_all, pool_all = T["x_all"], T["y_all"], T["kv_all"], T["pool_all"]
    cache_kv = T["cache_kv"]

    state = {"bank": 0, "tmp": 0, "st": 0, "ub": 0}

    def bank():
        b = state["bank"]
        state["bank"] = (b + 1) % 8
        return banks[b], bank_res[b]

    def tmp():
        i = state["tmp"]
        state["tmp"] = (i + 1) % len(T["tmps"])
        return T["tmps"][i], T["tmp_res"][i]

    def stat():
        i = state["st"]
        state["st"] = (i + 1) % len(T["stats"])
        return T["stats"][i], T["stat_res"][i]

    def mm(out, lhsT, rhs, start, stop, reads, writes, skip=False):
        P.op("pe", lambda e: e.matmul(out, lhsT, rhs, start=start, stop=stop, skip_group_check=skip), reads, writes)

    def tr(out, in_, idn, reads, writes):
        P.op("pe", lambda e: e.transpose(out, in_, idn), reads, writes)

    def act(out, in_, func, reads, writes, scale=1.0, accum=None):
        if accum is None:
            P.op("act", lambda e: e.activation(out=out, in_=in_, func=func, scale=scale), reads, writes)
        else:
            P.op("act", lambda e: e.activation(out=out, in_=in_, func=func, scale=scale, accum_out=accum), reads, writes)

    def tt(eng, out, in0, in1, op, reads, writes):
        P.op(eng, lambda e: e.tensor_tensor(out=out, in0=in0, in1=in1, op=op), reads, writes)

    def ts(eng, out, in0, s1, s2, op0, op1, reads, writes):
        if s2 is None:
            P.op(eng, lambda e: e.tensor_scalar(out=out, in0=in0, scalar1=s1, scalar2=None, op0=op0), reads, writes)
        else:
            P.op(eng, lambda e: e.tensor_scalar(out=out, in0=in0, scalar1=s1, scalar2=s2, op0=op0, op1=op1), reads, writes)

    def stt(out, in0, scalar, in1, op0, op1, reads, writes):
        P.op("dve", lambda e: e.scalar_tensor_tensor(out=out, in0=in0, scalar=scalar, in1=in1, op0=op0, op1=op1), reads, writes)

    def cp(eng, out, in_, reads, writes):
        P.op(eng, lambda e: e.tensor_copy(out=out, in_=in_), reads, writes)

    def red(out, in_, reads, writes):
        P.op("dve", lambda e: e.tensor_reduce(out=out, in_=in_, axis=AX.X, op=ALU.add), reads, writes)

    def recip(out, in_, reads, writes):
        P.op("dve", lambda e: e.reciprocal(out=out, in_=in_), reads, writes)

    def dma(eng, out, in_, reads, writes):
        P.op(eng, lambda e: e.dma_start(out=out, in_=in_), reads, writes, dma=True)

    def rstd_of(ss_ap, n, eps, width, reads_res, rows=128):
        a, ar = stat()
        ts("dve", a[0:rows, 0:width], ss_ap, 1.0 / n, eps, ALU.mult, ALU.add, reads_res, [ar])
        b, br = stat()
        act(b[0:rows, 0:width], a[0:rows, 0:width], AF.Sqrt, [ar], [br])
        c, cr = stat()
        recip(c[0:rows, 0:width], b[0:rows, 0:width], [br], [cr])
        return c, cr

    hbm = T["hbm_res"]
    lamw = T["lamw"]
    lam_t = T["lam_t"]
    t0, t0r = tmp()
    tt("dve", t0[:, 0:128], lamw[:, 0, :], lamw[:, 1, :], ALU.mult, [CONST], [t0r])
    tt("dve", t0[:, 128:256], lamw[:, 2, :], lamw[:, 3, :], ALU.mult, [t0r], [t0r])
    red(lam_t[:, 0:2], t0[:, 0:256].rearrange("p (a b) -> p a b", a=2), [t0r], [T["lam_res"]])
    act(lam_t[:, 0:2], lam_t[:, 0:2], AF.Exp, [T["lam_res"]], [T["lam_res"]])
    tt("dve", lam_t[:, 2:3], lam_t[:, 0:1], lam_t[:, 1:2], ALU.subtract, [T["lam_res"]], [T["lam_res"]])
    ts("dve", lam_t[:, 3:4], lam_t[:, 2:3], LAM_INIT, None, ALU.add, None, [T["lam_res"]], [T["lam_res"]])
    lam_ap = lam_t[:, 3:4]
    LAMR = T["lam_res"]
    gs4 = T["gs4"]
    ts("dve", gs4[:, :], T["gsT4"][:, :], 1.0 - LAM_INIT, None, ALU.mult, None, [CONST], [T["gs4_res"]])
    SH, SHo, HC = T["SH"], T["SHo"], T["HC"]
    P.op("dve", lambda e: e.memset(HC[:, :, :], 0.0), [], [T["hc_res"]])
    P.op("dve", lambda e: e.memset(T["ones"][:, :], 1.0), [], [T["ones_res"]])
    cp("dve", T["identb"][:, :], ident[:, :], [CONST], [T["identb_res"]])
    for seq in range(4):
        po, por = T["po"], T["po_res"]
        dma("sp", po[0:15, :], T["state_pool"][seq, :, :], [], [por])
        bk, bkr = bank()
        for c in range(16):
            tr(bk[:, c * 15:(c + 1) * 15], po[0:15, c * 128:(c + 1) * 128], ident[0:15, 0:15], [por, CONST], [bkr])
        cp("dve", SH[:, seq, :, :], bk[:, 0:240].rearrange("p (c t) -> p c t", c=16), [bkr], [T["sh_res"]])

    def transpose_rows_to_fm(src, src_res, nblk, dst_fn, scale_fn, dst_res_fn, rows=128, bank_fn=None):
        for j0 in range(0, nblk, 4):
            n = min(4, nblk - j0)
            bk, bkr = (bank_fn or bank)()
            for j in range(n):
                tr(bk[:, j * 128:j * 128 + rows], src[0:rows, (j0 + j) * 128:(j0 + j + 1) * 128], ident[0:rows, 0:rows], [src_res, CONST], [bkr])
            pv = bk[:, 0:n * 128].rearrange("p (a b) -> p a b", a=n)[:, :, 0:rows]
            sc = scale_fn(j0, n)
            wr = dst_res_fn(j0, n)
            if isinstance(sc, float):
                ts("dve", dst_fn(j0, n), pv, sc, None, ALU.mult, None, [bkr], wr)
            else:
                tt("dve", dst_fn(j0, n), pv, sc, ALU.mult, [bkr, CONST], wr)

    def proj_fm(w, nkc, col0, nchunks, rhs_fn, rhs_res_fn, evac, krow0=0, hooks=None):
        kslab = min(nkc, 16)
        ncol = SLOT_E // kslab
        ncol = min(ncol, 512, nchunks * 128)
        cpb = ncol // 128
        nks = nkc // kslab
        for cg in range(nchunks // cpb):
            if hooks and cg in hooks and not P.dry:
                hooks[cg]()
            bks = [bank() for _ in range(cpb)]
            for ks in range(nks):
                slab, sres = ws.get(w, krow0 + ks * kslab, kslab, col0 + cg * ncol, ncol)
                if P.dry:
                    continue
                for m in range(cpb):
                    for kc in range(kslab):
                        kk = ks * kslab + kc
                        mm(bks[m][0][:, 0:G], slab[:, kc, m * 128:(m + 1) * 128], rhs_fn(kk),
                           kk == 0, kk == nkc - 1, [sres, rhs_res_fn(kk)], [bks[m][1]])
            if P.dry:
                continue
            for m in range(cpb):
                evac(cg * cpb + m, bks[m][0], bks[m][1])

    def proj_tm(w, nkc, col0, ncb, lhs_fn, lhs_res_fn, evac):
        kslab = 8
        nks = nkc // kslab
        pending = None
        for cb in range(ncb):
            bks = [bank() for _ in range(NSUB)]
            for ks in range(nks):
                slab, sres = ws.get(w, ks * kslab, kslab, col0 + cb * 512, 512)
                if P.dry:
                    continue
                for s in range(NSUB):
                    for kc in range(kslab):
                        kk = ks * kslab + kc
                        mm(bks[s][0][:, :], lhs_fn(kk, s), slab[:, kc, :], kk == 0, kk == nkc - 1,
                           [sres, lhs_res_fn(kk)], [bks[s][1]])
                if ks == 0 and pending is not None:
                    pcb, pbks = pending
                    pending = None
                    for s in range(NSUB):
                        evac(pcb, s, pbks[s][0], pbks[s][1])
            if P.dry:
                continue
            pending = (cb, bks)
        if pending is not None:
            pcb, pbks = pending
            for s in range(NSUB):
                evac(pcb, s, pbks[s][0], pbks[s][1])

    xsq_res = T["xsq_res"]

    def norm_pre(src_rows_ap, src_res_fn, ss_known=None):
        xs = T["xs"][0]
        junk, jr = T["junk"], T["junk_res"]
        if ss_known is None:
            ssq, ssr = stat()
            for q in range(4):
                cs = slice(q * 1024, (q + 1) * 1024)
                dma("sp", xs[:, cs], src_rows_ap[:, cs], src_res_fn(q), [xsq_res[q]])
                act(junk[:, cs], xs[:, cs], AF.Square, [xsq_res[q]], [jr, ssr], accum=ssq[:, q:q + 1])
            tot, totr = stat()
            red(tot[:, 0:1], ssq[:, 0:4], [ssr], [totr])
        else:
            ssq, ssr = ss_known
            tot, totr = stat()
            red(tot[:, 0:1], ssq[:, 0:8], [ssr], [totr])
            for q in range(4):
                cs = slice(q * 1024, (q + 1) * 1024)
                dma("sp", xs[:, cs], src_rows_ap[:, cs], src_res_fn(q), [xsq_res[q]])
        rs, rsr = rstd_of(tot[:, 0:1], float(D), EPS, 1, [totr])
        for q in range(4):
            cs = slice(q * 1024, (q + 1) * 1024)
            ts("dve", xs[:, cs], xs[:, cs], rs[:, 0:1], None, ALU.mult, None, [xsq_res[q], rsr], [xsq_res[q]])

    def norm_post(s, gT):
        xs = T["xs"][0]
        for q in range(4):
            transpose_rows_to_fm(
                xs[:, q * 1024:(q + 1) * 1024], xsq_res[q], 8,
                lambda j0, n, q=q: B1[:, q * 8 + j0:q * 8 + j0 + n, s * 128:(s + 1) * 128],
                lambda j0, n, q=q: gT[:, q * 8 + j0:q * 8 + j0 + n].unsqueeze(2).to_broadcast([128, n, 128]),
                lambda j0, n, q=q: [b1_res[q * 8 + j] for j in range(j0, j0 + n)])

    identb = T["identb"]
    IDB = T["identb_res"]

    def attention_group(sts):
        osb, osr = T["osb"], T["osb_res"]
        OA, OAr = banks[0], bank_res[0]
        OB, OBr = banks[1], bank_res[1]
        SBk, SBr = banks[2], bank_res[2]
        its = []
        for s, st in enumerate(sts):
            if st < 16:
                segs = [(0, 128, [(kv_all[j * 128:(j + 1) * 128, :, :], 128, j == st, j) for j in range(st + 1)])]
            else:
                segs = []
                for i in range(2):
                    seq = (st - 16) * 2 + i
                    blks = [(cache_kv[seq, j * 128:(j + 1) * 128, :, :], 128, False, None) for j in range(16)]
                    r0 = st * 128 + i * 64
                    blks.append((kv_all[r0:r0 + 64, :, :], 64, False, st))
                    segs.append((i * 64, 64, blks))
            for (q0, Lq, blks) in segs:
                for hp in range(4):
                    for bi, (src, nk, diag, hst) in enumerate(blks):
                        its.append(dict(s=s, q0=q0, Lq=Lq, hp=hp, bi=bi, nb=len(blks), src=src, nk=nk, diag=diag, hst=hst))
        n = len(its)

        def LD(i):
            it = its[i]
            kv, kvr = T["kvb"][i % NKV], T["kvb_res"][i % NKV]
            hp, nk = it["hp"], it["nk"]
            rd = [hbm[("k", it["hst"], hp)], hbm[("v", it["hst"], hp)]] if it["hst"] is not None else []
            dma("pool", kv[0:nk, :, :], it["src"][:, :, hp * 512:(hp + 1) * 512], rd, [kvr])

        def TC(i):
            it = its[i]
            kv, kvr = T["kvb"][i % NKV], T["kvb_res"][i % NKV]
            kt, ktr = T["ktb"][i % 3], T["ktb_res"][i % 3]
            nk = it["nk"]
            tb, tbr = banks[3 + i % 2], bank_res[3 + i % 2]
            tbv = tb[:, :].bitcast(BF16)
            for u in range(4):
                tr(tbv[:, u * 128:u * 128 + nk], kv[0:nk, 0, u * 128:(u + 1) * 128], identb[0:nk, 0:nk], [kvr, IDB], [tbr])
            cp("dve", kt[:, :, 0:nk], tbv[:, 0:512].rearrange("p (a b) -> p a b", a=4)[:, :, 0:nk], [tbr], [ktr])

        def LE(i):
            it = its[i]
            kt, ktr = T["ktb"][i % 3], T["ktb_res"][i % 3]
            pt, ptr_ = T["ptb"][i % 3], T["ptb_res"][i % 3]
            nk, Lq, hp = it["nk"], it["Lq"], it["hp"]
            qc0 = it["s"] * 128 + it["q0"]
            lg, lgr = banks[5 + i % 2], bank_res[5 + i % 2]
            for u in range(4):
                mm(lg[0:nk, u * 128:u * 128 + Lq], kt[:, u, 0:nk], BIG[:, QT0 + hp * 4 + u, qc0:qc0 + Lq],
                   True, True, [ktr, big_res[QT0 + hp * 4 + u]], [lgr])
            act(pt[0:nk, :, 0:Lq], lg[0:nk, :].rearrange("p (a b) -> p a b", a=4)[:, :, 0:Lq], AF.Exp, [lgr], [ptr_])
            if it["diag"]:
                P.op("dve", lambda e, a=pt[64:128, :, 0:64]: e.memset(a, 0.0), [], [ptr_])

        def PV(i):
            it = its[i]
            kv, kvr = T["kvb"][i % NKV], T["kvb_res"][i % NKV]
            pt, ptr_ = T["ptb"][i % 3], T["ptb_res"][i % 3]
            nk, Lq, hp, bi, nb = it["nk"], it["Lq"], it["hp"], it["bi"], it["nb"]
            for u in range(4):
                ob, obr = (OA, OAr) if u < 2 else (OB, OBr)
                mm(ob[0:Lq, (u % 2) * 256:(u % 2 + 1) * 256], pt[0:nk, u, 0:Lq], kv[0:nk, 1, (u // 2) * 256:(u // 2 + 1) * 256],
                   bi == 0 and u % 2 == 0, bi == nb - 1, [ptr_, kvr], [obr], skip=True)
            for u in range(4):
                mm(SBk[0:Lq, u:u + 1], pt[0:nk, u, 0:Lq], T["ones"][0:nk, 0:1],
                   bi == 0 and u == 0, bi == nb - 1, [ptr_, T["ones_res"]], [SBr], skip=True)
            if bi != nb - 1:
                return
            rsm, rsmr = stat()
            recip(rsm[0:Lq, 0:4], SBk[0:Lq, 0:4], [SBr], [rsmr])
            cl, clr = stat()
            ts("dve", cl[0:Lq, 0:4], rsm[0:Lq, 0:4], lam_ap[0:Lq, :], None, ALU.mult, None, [rsmr, LAMR], [clr])
            for j in range(2):
                ob, obr = (OA, OAr) if j == 0 else (OB, OBr)
                t1, t1r = tmp()
                act(t1[0:Lq, 0:256], ob[0:Lq, 256:512], AF.Copy, [obr, clr], [t1r], scale=cl[0:Lq, 2 * j + 1:2 * j + 2])
                h = hp * 2 + j
                stt(osb[0:Lq, h * 256:(h + 1) * 256], ob[0:Lq, 0:256], rsm[0:Lq, 2 * j:2 * j + 1],
                    t1[0:Lq, 0:256], ALU.mult, ALU.subtract, [obr, rsmr, t1r], [osr])
            if hp != 3:
                return
            qc0 = it["s"] * 128 + it["q0"]
            jk = B1[:, 0:8, :].rearrange("p a b -> p (a b)")
            jkr = [b1_res[j] for j in range(8)]
            act(jk[0:Lq, 0:2048], osb[0:Lq, :], AF.Square, [osr], jkr)
            ss8, ss8r = stat()
            red(ss8[0:Lq, 0:8], jk[0:Lq, 0:2048].rearrange("p (h e) -> p h e", h=8), jkr, [ss8r])
            rs8, rs8r = rstd_of(ss8[0:Lq, 0:8], 256.0, SUB_EPS, 8, [ss8r], rows=Lq)
            tt("dve", osb[0:Lq, :].rearrange("p (h e) -> p h e", h=8), osb[0:Lq, :].rearrange("p (h e) -> p h e", h=8),
               rs8[0:Lq, 0:8].unsqueeze(2).to_broadcast([Lq, 8, 256]), ALU.mult, [osr, rs8r], [osr])
            deferred.append((i + 2, lambda: transpose_rows_to_fm(
                osb, osr, 16,
                lambda j0, n: BIG[:, OT0 + j0:OT0 + j0 + n, qc0:qc0 + Lq],
                lambda j0, n: gs4[:, 0:n].unsqueeze(2).to_broadcast([128, n, Lq]),
                lambda j0, n: [big_res[OT0 + j] for j in range(j0, j0 + n)], rows=Lq,
                bank_fn=lambda: (banks[7], bank_res[7]))))

        deferred = []
        LDA = NKV - 1
        for i in range(min(LDA, n)):
            LD(i)
        for i in range(min(2, n)):
            TC(i)
        if n > 0:
            LE(0)
        for t in range(n):
            if t + LDA < n:
                LD(t + LDA)
            if t + 2 < n:
                TC(t + 2)
            if t + 1 < n:
                LE(t + 1)
            while deferred and deferred[0][0] <= t:
                deferred.pop(0)[1]()
            if deferred:
                assert not (its[t]["bi"] == its[t]["nb"] - 1), "combine would overwrite osb before its transposes"
            PV(t)
        while deferred:
            deferred.pop(0)[1]()

    ngroups = NST // NSUB
    for g in range(ngroups):
        sts = [g * NSUB + s for s in range(NSUB)]
        is_sample = sts[0] >= 16
        if P.dry:
            pass
        def a1_pre(st):
            norm_pre(x_all[st * 128:(st + 1) * 128, :], lambda q: [])

        def a1_rest(sts_n, first_pre_done):
            for s_, st_ in enumerate(sts_n):
                if not (s_ == 0 and first_pre_done):
                    a1_pre(st_)
                norm_post(s_, T["g1T"])
        if not P.dry and g == 0:
            a1_rest(sts, False)

        def evac_qkv(cb, s, bk, bkr):
            st = sts[s]
            rows = slice(st * 128, (st + 1) * 128)
            if cb < 8:
                sq, sqr = tmp()
                act(sq[:, :], bk[:, :], AF.Square, [bkr], [sqr])
                ss4, ss4r = stat()
                red(ss4[:, 0:4], sq[:, :].rearrange("p (u d) -> p u d", u=4), [sqr], [ss4r])
                rs4, rs4r = rstd_of(ss4[:, 0:4], 128.0, EPS, 4, [ss4r])
                t1, t1r = tmp()
                tt("dve", t1[:, :].rearrange("p (u d) -> p u d", u=4), bk[:, :].rearrange("p (u d) -> p u d", u=4),
                   rs4[:, 0:4].unsqueeze(2).to_broadcast([128, 4, 128]), ALU.mult, [bkr, rs4r], [t1r])
                t2, t2r = tmp()
                gb_ = T["gq_b"] if cb < 4 else T["gk_b"]
                tt("dve", t2[:, :], t1[:, :], gb_[:, :], ALU.mult, [t1r, CONST], [t2r])
                if cb < 4:
                    transpose_rows_to_fm(
                        t2, t2r, 4,
                        lambda j0, n: BIG[:, QT0 + cb * 4 + j0:QT0 + cb * 4 + j0 + n, s * 128:(s + 1) * 128],
                        lambda j0, n: float(128 ** -0.5),
                        lambda j0, n: [big_res[QT0 + cb * 4 + j] for j in range(j0, j0 + n)])
                else:
                    hp = cb - 4
                    dma("sp", kv_all[rows, 0, hp * 512:(hp + 1) * 512], t2[:, :], [t2r], [hbm[("k", st, hp)]])
            else:
                hp = cb - 8
                t1, t1r = tmp()
                act(t1[:, :], bk[:, :], AF.Copy, [bkr], [t1r])
                dma("sp", kv_all[rows, 1, hp * 512:(hp + 1) * 512], t1[:, :], [t1r], [hbm[("v", st, hp)]])

        proj_tm(T["w_in"], 32, 0, 12, lambda kk, s: B1[:, kk, s * 128:(s + 1) * 128], lambda kk: b1_res[kk], evac_qkv)

        if is_sample:
            nseg, L = 4, 64
        else:
            nseg, L = 1, G
        W_ = 15 + L

        def evac_ugg(chunk, bk, bkr):
            if chunk < 16:
                c = chunk
                gi = c // 4
                w = WINS[gi]
                i = state["ub"]
                state["ub"] = (i + 1) % 2
                ub, ubr = T["ub"][i], T["ub_res"][i]
                sa, sar = T["sa"][i], T["sa_res"][i]
                sb_, sbr = T["sb"][i], T["sb_res"][i]
                ubv = ub[:, 0:nseg * W_].rearrange("p (a b) -> p a b", a=nseg)
                sav = sa[:, 0:nseg * W_].rearrange("p (a b) -> p a b", a=nseg)
                sbv = sb_[:, 0:nseg * W_].rearrange("p (a b) -> p a b", a=nseg)
                if is_sample:
                    cp("dve", ubv[:, :, 0:15], SH[:, :, c, :], [T["sh_res"]], [ubr])
                else:
                    cp("dve", ubv[:, :, 0:15], HC[:, c:c + 1, :], [T["hc_res"]], [ubr])
                act(ubv[:, :, 15:W_], bk[:, 0:G].rearrange("p (a b) -> p a b", a=nseg), AF.Copy, [bkr], [ubr])
                cur, curr = ubv, ubr
                outs = [(sav, sar), (sbv, sbr)]
                for lev in range(gi + 1):
                    sh = 1 << lev
                    lo = (1 << (lev + 1)) - 1
                    nxt, nxtr = outs[lev % 2]
                    tt("dve", nxt[:, :, lo:W_], cur[:, :, lo:W_], cur[:, :, lo - sh:W_ - sh], ALU.add, [curr], [nxtr])
                    cur, curr = nxt, nxtr
                pl = BIG[:, PLD0 + c, :].rearrange("p (a b) -> p a b", a=nseg)
                stt(pl, cur[:, :, 15:W_], 1.0 / w, ubv[:, :, 15:W_], ALU.mult, ALU.subtract, [curr, ubr], [big_res[PLD0 + c]])
                if (not is_sample) and g == 0:
                    t1, t1r = stat()
                    tt("dve", t1[:, 0:15], cur[:, 0, 15:30], T["invc"][:, gi, 0:15], ALU.mult, [curr, CONST], [t1r])
                    tt("dve", BIG[:, PLD0 + c, 0:15], t1[:, 0:15], ubv[:, 0, 15:30], ALU.subtract, [t1r, ubr], [big_res[PLD0 + c]])
                if is_sample:
                    cp("dve", SHo[:, :, c, :], ubv[:, :, L:L + 15], [ubr], [T["sho_res"]])
                else:
                    cp("dve", HC[:, c:c + 1, :], ubv[:, :, L:L + 15], [ubr], [T["hc_res"]])
            elif chunk < 48:
                c = chunk - 16
                act(BIG[:, SGA0 + c, :], bk[:, 0:G], AF.Sigmoid, [bkr], [big_res[SGA0 + c]])
            else:
                c = chunk - 48
                act(BIG[:, SGB0 + c, :], bk[:, 0:G], AF.Sigmoid, [bkr], [big_res[SGB0 + c]])

        proj_fm(T["w_in"], 32, 6144, 80, lambda kk: B1[:, kk, :], lambda kk: b1_res[kk], evac_ugg)

        if not P.dry:
            def pool_out(src3, src_res, seq_out):
                po, por = T["po"], T["po_res"]
                for c4 in range(4):
                    bk, bkr = bank()
                    for j in range(4):
                        c = c4 * 4 + j
                        tr(bk[0:15, j * 128:(j + 1) * 128], src3(c), ident[:, :], [src_res, CONST], [bkr])
                    cp("dve", po[0:15, c4 * 512:(c4 + 1) * 512], bk[0:15, :], [bkr], [por])
                dma("sp", pool_all[seq_out, :, :], po[0:15, :], [por], [])
            if sts[-1] == 15:
                pool_out(lambda c: HC[:, c, :], T["hc_res"], 0)
            if is_sample:
                for seq in range(4):
                    pool_out(lambda c, seq=seq: SHo[:, seq, c, :], T["sho_res"], 1 + seq)

        for gi in range(4):
            def evac_mix(m, bk, bkr, gi=gi):
                c = gi * 4 + m
                ts("dve", BIG[:, MIX0 + c, :], bk[:, 0:G], T["psT"][:, c:c + 1], None, ALU.mult, None, [bkr, CONST], [big_res[MIX0 + c]])
            proj_fm(T["w_pool"], 4, 0, 4, lambda kk, gi=gi: BIG[:, PLD0 + gi * 4 + kk, :], lambda kk, gi=gi: big_res[PLD0 + gi * 4 + kk],
                    evac_mix, krow0=gi * 4)

        def evac_yb(c, bk, bkr):
            tt("dve", BIG[:, SGB0 + c, :], bk[:, 0:G], BIG[:, SGB0 + c, :], ALU.mult, [bkr, big_res[SGB0 + c]], [big_res[SGB0 + c]])
        proj_fm(T["w_pool_out"], 16, 0, 32, lambda kk: BIG[:, MIX0 + kk, :], lambda kk: big_res[MIX0 + kk], evac_yb)

        if not P.dry:
            attention_group(sts)

        def evac_ya(c, bk, bkr):
            t1, t1r = tmp()
            tt("dve", t1[:, 0:G], bk[:, 0:G], BIG[:, SGA0 + c, :], ALU.mult, [bkr, big_res[SGA0 + c]], [t1r])
            tt("dve", B1[:, c, :], t1[:, 0:G], BIG[:, SGB0 + c, :], ALU.add, [t1r, big_res[SGB0 + c]], [b1_res[c]])
        proj_fm(T["w_attn_out"], 16, 0, 32, lambda kk: BIG[:, OT0 + kk, :], lambda kk: big_res[OT0 + kk], evac_ya)

        def evac_wo(cb, s, bk, bkr):
            st = sts[s]
            rows = slice(st * 128, (st + 1) * 128)
            xb, xbr = tmp()
            dma("sp", xb[:, :], x_all[rows, cb * 512:(cb + 1) * 512], [], [xbr])
            t1, t1r = tmp()
            tt("dve", t1[:, :], bk[:, :], xb[:, :], ALU.add, [bkr, xbr], [t1r])
            act(xb[:, :], t1[:, :], AF.Square, [t1r], [xbr, T["ssx_res"][s]], accum=T["ssx"][:, s * 8 + cb:s * 8 + cb + 1])
            dma("sp", y_all[rows, cb * 512:(cb + 1) * 512], t1[:, :], [t1r], [hbm[("y", st, cb)]])
        proj_tm(T["w_o"], 32, 0, 8, lambda kk, s: B1[:, kk, s * 128:(s + 1) * 128], lambda kk: b1_res[kk], evac_wo)

        if not P.dry:
            for s, st in enumerate(sts):
                norm_pre(y_all[st * 128:(st + 1) * 128, :], lambda q, st=st: [hbm[("y", st, 2 * q)], hbm[("y", st, 2 * q + 1)]],
                         ss_known=(T["ssx"][:, s * 8:(s + 1) * 8], T["ssx_res"][s]))
                norm_post(s, T["g2T"])

        def evac_up(c, bk, bkr):
            t1, t1r = tmp()
            act(t1[:, 0:G], bk[:, 0:G], AF.Relu, [bkr], [t1r])
            tt("dve", BIG[:, c, :], t1[:, 0:G], t1[:, 0:G], ALU.mult, [t1r], [big_res[c]])
        nxt = [(g + 1) * NSUB + s_ for s_ in range(NSUB)] if g + 1 < ngroups else None
        proj_fm(T["w_up"], 32, 0, 128, lambda kk: B1[:, kk, :], lambda kk: b1_res[kk], evac_up,
                hooks=({32: (lambda: a1_pre(nxt[0]))} if nxt else None))
        if nxt and not P.dry:
            a1_rest(nxt, True)

        def evac_down(cb, s, bk, bkr):
            st = sts[s]
            rows = slice(st * 128, (st + 1) * 128)
            xb, xbr = tmp()
            dma("sp", xb[:, :], y_all[rows, cb * 512:(cb + 1) * 512], [hbm[("y", st, cb)]], [xbr])
            t1, t1r = tmp()
            tt("dve", t1[:, :], bk[:, :], xb[:, :], ALU.add, [bkr, xbr], [t1r])
            dma("sp", y_all[rows, cb * 512:(cb + 1) * 512], t1[:, :], [t1r], [hbm[("y", st, cb)]])
        proj_tm(T["w_down"], 128, 0, 8, lambda kk, s: BIG[:, kk, s * 128:(s + 1) * 128], lambda kk: big_res[kk], evac_down)


def build_nc():
    nc = bass.Bass("TRN2", target_bir_lowering=False)
    T = {}

    def din(name, shape):
        return nc.dram_tensor(name, list(shape), F32, kind="ExternalInput").ap()

    def dout(name, shape):
        return nc.dram_tensor(name, list(shape), F32, kind="ExternalOutput").ap()

    T["x_all"] = din("x_all", (TOK, D))
    T["cache_kv"] = din("cache_kv", (4, 2048, 2, 2048))
    T["state_pool"] = din("state_pool", (4, 15, 2048))
    T["w_in"] = din("w_in", (4096, 16384))
    T["w_attn_out"] = din("w_attn_out", (2048, 4096))
    T["w_pool"] = din("w_pool", (2048, 512))
    T["w_pool_out"] = din("w_pool_out", (2048, 4096))
    T["w_o"] = din("w_o", (4096, 4096))
    T["w_up"] = din("w_up", (4096, 16384))
    T["w_down"] = din("w_down", (16384, 4096))
    d_g1T = din("g1T", (128, 32))
    d_g2T = din("g2T", (128, 32))
    d_psT = din("psT", (128, 16))
    d_gsT4 = din("gsT4", (128, 4))
    d_gq = din("gq_b", (128, 512))
    d_gk = din("gk_b", (128, 512))
    d_lam = din("lamw", (128, 512))
    d_ident = din("ident", (128, 128))
    d_invc = din("invc", (128, 64))
    T["y_all"] = dout("y_all", (TOK, D))
    T["kv_all"] = dout("kv_all", (TOK, 2, 2048))
    T["pool_all"] = dout("pool_all", (5, 15, 2048))

    with ExitStack() as es:
        def sb(name, shape, dt):
            return es.enter_context(nc.sbuf_tensor("sb_" + name, list(shape), dt))

        slots = [sb(f"slot{i}", (128, SLOT_E), BF16) for i in range(NSLOT)]
        slot_res = [Res(f"slot{i}") for i in range(NSLOT)]
        T["B1"] = sb("B1", (128, 32, G), BF16)
        T["b1_res"] = [Res(f"b1_{i}") for i in range(32)]
        T["BIG"] = sb("BIG", (128, 128, G), BF16)
        T["big_res"] = [Res(f"big_{i}") for i in range(128)]
        T["xs"] = [sb("xs0", (128, D), F32)]
        T["osb"] = sb("osb", (128, 2048), F32)
        T["osb_res"] = Res("osb")
        T["junk"] = T["osb"][:, :].bitcast(BF16)
        T["junk_res"] = T["osb_res"]
        T["xsq_res"] = [Res(f"xsq{i}") for i in range(4)]
        T["ssx"] = sb("ssx", (128, NSUB * 8), F32)
        T["ssx_res"] = [Res(f"ssx{i}") for i in range(NSUB)]
        T["po"] = T["osb"]
        T["po_res"] = T["osb_res"]
        NT = 6
        T["tmps"] = [sb(f"tmp{i}", (128, 512), F32) for i in range(NT)]
        T["tmp_res"] = [Res(f"tmp{i}") for i in range(NT)]
        NS = 16
        T["stats"] = [sb(f"stat{i}", (128, 16), F32) for i in range(NS)]
        T["stat_res"] = [Res(f"stat{i}") for i in range(NS)]
        T["kvb"] = [sb(f"kvb{i}", (128, 2, 512), BF16) for i in range(NKV)]
        T["kvb_res"] = [Res(f"kvb{i}") for i in range(NKV)]
        T["ktb"] = [sb(f"ktb{i}", (128, 4, 128), BF16) for i in range(3)]
        T["ktb_res"] = [Res(f"ktb{i}") for i in range(3)]
        T["ptb"] = [sb(f"ptb{i}", (128, 4, 128), BF16) for i in range(3)]
        T["ptb_res"] = [Res(f"ptb{i}") for i in range(3)]
        T["identb"] = sb("identb", (128, 128), BF16)
        T["identb_res"] = Res("identb", const=True)
        UBW = 320
        for nm in ("ub", "sa", "sb"):
            T[nm] = [sb(f"{nm}{i}", (128, UBW), F32) for i in range(2)]
            T[nm + "_res"] = [Res(f"{nm}{i}") for i in range(2)]
        T["SH"] = sb("SH", (128, 4, 16, 15), F32)
        T["sh_res"] = Res("sh")
        T["SHo"] = T["SH"]
        T["sho_res"] = T["sh_res"]
        T["HC"] = sb("HC", (128, 16, 15), F32)
        T["hc_res"] = Res("hc")
        T["ones"] = sb("ones", (128, 2), BF16)
        T["ones_res"] = Res("ones")
        T["lam_t"] = sb("lam_t", (128, 4), F32)
        T["lam_res"] = Res("lam")
        T["gs4"] = sb("gs4", (128, 4), F32)
        T["gs4_res"] = Res("gs4")
        T["g1T"] = sb("g1T", (128, 32), F32)
        T["g2T"] = sb("g2T", (128, 32), F32)
        T["psT"] = sb("psT", (128, 16), F32)
        T["gsT4"] = sb("gsT4", (128, 4), F32)
        T["gq_b"] = sb("gq_b", (128, 512), F32)
        T["gk_b"] = sb("gk_b", (128, 512), F32)
        T["lamw"] = sb("lamw", (128, 4, 128), F32)
        T["ident"] = sb("ident", (128, 128), F32)
        T["invc"] = sb("invc", (128, 4, 16), F32)
        T["const_res"] = Res("const", const=True)
        banks = [es.enter_context(nc.psum_tensor(f"bank{i}", [128, 512], F32)) for i in range(8)]
        T["banks"] = banks
        T["bank_res"] = [Res(f"bank{i}") for i in range(8)]
        hbm = {}
        for st in range(NST):
            for hp in range(4):
                hbm[("k", st, hp)] = Res(f"k{st}_{hp}")
                hbm[("v", st, hp)] = Res(f"v{st}_{hp}")
            for cb in range(8):
                hbm[("y", st, cb)] = Res(f"y{st}_{cb}")
        T["hbm_res"] = hbm

        Pd = Prog()
        Pd.dry = True
        wsd = WStream(Pd, slots, slot_res, None)
        build_core_program(nc, T, Pd, wsd)
        sched = wsd.rec

        P = Prog()
        CONST = T["const_res"]
        cl = [(T["g1T"][:, :], d_g1T), (T["g2T"][:, :], d_g2T), (T["psT"][:, :], d_psT), (T["gsT4"][:, :], d_gsT4),
              (T["gq_b"][:, :], d_gq), (T["gk_b"][:, :], d_gk), (T["lamw"][:, :, :], d_lam.rearrange("p (a b) -> p a b", a=4)),
              (T["ident"][:, :], d_ident), (T["invc"][:, :, :], d_invc.rearrange("p (a b) -> p a b", a=4))]
        cres = [Res(f"c{i}") for i in range(len(cl))]
        for (dst, src), r in zip(cl, cres):
            P.op("sp", lambda e, d=dst, s=src: e.dma_start(out=d, in_=s), [], [r], dma=True)
        P.op("dve", lambda e: e.memset(T["gs4"][:, :], 0.0), cres, [CONST, T["gs4_res"]])
        npass = len(sched) // (NST // NSUB)
        scratch = [nc.dram_tensor(f"wscr{i}", [240 * 128, SLOT_E], BF16, kind="Internal").ap()
                   for i in range((npass + 239) // 240)]
        ws = WStream(P, slots, slot_res, sched, scratch, npass)
        build_core_program(nc, T, P, ws)
        P.finish()
        import os
        if os.environ.get("KDEBUG"):
            print("ops", {e: len(P.ops[e]) for e in P.ENG}, "slabs", len(sched), "sbuf left", nc.sbuf_bytes_remaining, flush=True)

        sem_names = {e: es.enter_context(nc.semaphore(f"s_{e}")) for e in Prog.ENG}
        dma_sems = {}
        for e in ("sp", "pool"):
            for i in range(P.ndma):
                dma_sems[(e, i)] = es.enter_context(nc.semaphore(f"d_{e}{i}"))
        with nc.Block() as block:
            P.emit(nc, block, sem_names, dma_sems)
    return nc


_NC_CACHE = {}


def _w_pool_rows(w_pool):
    return np.ascontiguousarray(w_pool.reshape(2048, 512))


def kernel(x_prompt, x_sample, cache_k, cache_v, state_pool, g_norm1, w_in, g_q, g_k, lambda_q1,
           lambda_k1, lambda_q2, lambda_k2, g_subln, w_attn_out, w_pool, pool_scale, w_pool_out, w_o,
           g_norm2, w_up, w_down):
    f = np.float32
    x_prompt = np.asarray(x_prompt, f)
    x_sample = np.asarray(x_sample, f)
    cache_k = np.asarray(cache_k, f)
    cache_v = np.asarray(cache_v, f)
    state_pool = np.asarray(state_pool, f)
    if "nc" not in _NC_CACHE:
        _NC_CACHE["nc"] = build_nc()
    nc = _NC_CACHE["nc"]

    def fmT(v, n):
        return np.ascontiguousarray(np.asarray(v, f).reshape(n, 128).T)

    g1T = fmT(g_norm1[0], 32)
    g2T = fmT(g_norm2[0], 32)
    psT = fmT(pool_scale[0], 16)
    gs = np.asarray(g_subln[0], f).reshape(2, 128).T
    gsT4 = np.ascontiguousarray(np.concatenate([gs, gs], axis=1))
    gq_b = np.ascontiguousarray(np.broadcast_to(np.tile(np.asarray(g_q[0], f), 4)[None, :], (128, 512)))
    gk_b = np.ascontiguousarray(np.broadcast_to(np.tile(np.asarray(g_k[0], f), 4)[None, :], (128, 512)))
    lamw = np.concatenate([np.asarray(a[0], f) for a in (lambda_q1, lambda_k1, lambda_q2, lambda_k2)])
    lamw = np.ascontiguousarray(np.broadcast_to(lamw[None, :], (128, 512)))
    ident = np.eye(128, dtype=f)
    invc = np.zeros((4, 16), f)
    for gi, w in enumerate(WINS):
        for t in range(16):
            invc[gi, t] = 1.0 / min(t + 1, w)
    invc = np.ascontiguousarray(np.broadcast_to(invc.reshape(1, 64), (128, 64)))
    shared = {
        "w_in": np.asarray(w_in[0], f), "w_attn_out": np.asarray(w_attn_out[0], f),
        "w_pool": _w_pool_rows(np.asarray(w_pool[0], f)), "w_pool_out": np.asarray(w_pool_out[0], f),
        "w_o": np.asarray(w_o[0], f), "w_up": np.asarray(w_up[0], f), "w_down": np.asarray(w_down[0], f),
        "g1T": g1T, "g2T": g2T, "psT": psT, "gsT4": gsT4, "gq_b": gq_b, "gk_b": gk_b, "lamw": lamw,
        "ident": ident, "invc": invc,
    }
    in_maps = []
    for c in range(8):
        m = dict(shared)
        m["x_all"] = np.ascontiguousarray(np.concatenate([x_prompt[c], x_sample[4 * c:4 * c + 4].reshape(256, D)], axis=0))
        m["cache_kv"] = np.ascontiguousarray(np.stack(
            [cache_k[0, 4 * c:4 * c + 4].reshape(4, 2048, 2048), cache_v[0, 4 * c:4 * c + 4].reshape(4, 2048, 2048)], axis=2))
        m["state_pool"] = np.ascontiguousarray(state_pool[0, 4 * c:4 * c + 4])
        in_maps.append(m)
    res = run_bass_kernel_spmd(nc, in_maps, core_ids=list(range(8)))
    R = res.results
    y_p = np.stack([R[c]["y_all"][:2048] for c in range(8)])
    y_s = np.concatenate([R[c]["y_all"][2048:].reshape(4, 64, D) for c in range(8)])
    k_p = np.stack([R[c]["kv_all"][:2048, 0] for c in range(8)]).reshape(1, 8, 2048, 8, 2, 128)
    v_p = np.stack([R[c]["kv_all"][:2048, 1] for c in range(8)]).reshape(1, 8, 2048, 8, 256)
    k_s = np.concatenate([R[c]["kv_all"][2048:, 0].reshape(4, 64, 2048) for c in range(8)]).reshape(1, 32, 64, 8, 2, 128)
    v_s = np.concatenate([R[c]["kv_all"][2048:, 1].reshape(4, 64, 2048) for c in range(8)]).reshape(1, 32, 64, 8, 256)
    p_p = np.stack([R[c]["pool_all"][0] for c in range(8)]).reshape(1, 8, 15, 2048)
    p_s = np.concatenate([R[c]["pool_all"][1:5] for c in range(8)]).reshape(1, 32, 15, 2048)
    return (y_p.astype(f), y_s.astype(f), k_p.astype(f), v_p.astype(f), p_p.astype(f),
            k_s.astype(f), v_s.astype(f), p_s.astype(f))
```
